# Optimizing a Trainium2 kernel written in Bass

```python
import jax, jax.numpy as jnp
from jax import lax
import numpy as np

D_MODEL = 1024
BATCH = 4
SEQ = 4096
DEPTH = 4

CTX_LEN = 256
GRID_W = 64
ROPE_THETA = 10000.0
EPS = 1e-6
BLOCK = 128
HEAD_DIM = 64
D_MIX = D_MODEL
MLA_HEADS = 4
MLA_NOPE = 64
MLA_ROPE = 32
MLA_V = 64
MLA_Q_RANK = 256
MLA_KV_RANK = 128
GQA_HEADS = 4
GQA_KV_HEADS = 2
SWA_HEADS = 4
SWA_KV_HEADS = 2
WINDOW = 128
SSD_HEADS = 4
SSD_HEAD_DIM = 64
SSD_GROUPS = 2
SSD_STATE = 128
SSD_CONV = 5
SSD_CHUNK = 128
SSD_INNER = SSD_HEADS * SSD_HEAD_DIM
SSD_CONV_DIM = SSD_INNER + 2 * SSD_GROUPS * SSD_STATE
A_WIDTHS = (MLA_Q_RANK, MLA_KV_RANK, MLA_ROPE, MLA_HEADS * MLA_V)
B_WIDTHS = (GQA_HEADS * HEAD_DIM, GQA_KV_HEADS * HEAD_DIM, GQA_KV_HEADS * HEAD_DIM, GQA_HEADS * HEAD_DIM)
C_WIDTHS = (SWA_HEADS * HEAD_DIM, SWA_KV_HEADS * HEAD_DIM, SWA_KV_HEADS * HEAD_DIM, SWA_HEADS * HEAD_DIM)
D_WIDTHS = (SSD_INNER, SSD_CONV_DIM, 2 * SSD_HEADS)
IN_WIDTHS = A_WIDTHS + B_WIDTHS + C_WIDTHS + D_WIDTHS
IN_COLS = sum(IN_WIDTHS)

kernel_name = 'hybrid_head_group_flow_block'


def rmsnorm(x, w):
    xf = x.astype(jnp.float32)
    y = xf * lax.rsqrt(jnp.mean(xf * xf, axis=-1, keepdims=True) + EPS)
    return (y * w.astype(jnp.float32)).astype(x.dtype)


def split_cols(p, widths):
    return jnp.split(p, [int(i) for i in np.cumsum(widths)[:-1]], axis=-1)


def axial_rope(n_tokens, rot_dim):
    rows = n_tokens // GRID_W
    row = jnp.repeat(jnp.arange(rows, dtype=jnp.float32), GRID_W)
    col = jnp.tile(jnp.arange(GRID_W, dtype=jnp.float32), rows)
    n_freq = rot_dim // 4
    inv_freq = ROPE_THETA ** (-jnp.arange(n_freq, dtype=jnp.float32) / n_freq)
    ang = jnp.concatenate([row[:, None] * inv_freq, col[:, None] * inv_freq], axis=-1)
    return jnp.cos(ang), jnp.sin(ang)


def apply_rope(x, cos, sin):
    half = x.shape[-1] // 2
    x1, x2 = x[..., :half], x[..., half:]
    c, s = cos[:, None, :], sin[:, None, :]
    return jnp.concatenate([x1 * c - x2 * s, x1 * s + x2 * c], axis=-1).astype(x.dtype)


def blocked_attention(q, k, v, scale):
    bsz, n = q.shape[:2]
    nblk = n // BLOCK
    qb = q.reshape(bsz, nblk, BLOCK, *q.shape[2:]).swapaxes(0, 1)

    def one_block(q_blk):
        s = jnp.einsum('bqhgd,bkhd->bhgqk', q_blk, k).astype(jnp.float32) * scale
        p = jax.nn.softmax(s, axis=-1).astype(v.dtype)
        return jnp.einsum('bhgqk,bkhd->bqhgd', p, v)

    out = lax.map(one_block, qb)
    return out.swapaxes(0, 1).reshape(bsz, n, -1)


def dense_sink_attention(q, k, v, sink, scale):
    bsz, n, kvh, g, _ = q.shape
    s = jnp.einsum('bqhgd,bkhd->bhgqk', q, k).astype(jnp.float32) * scale
    sink_col = jnp.broadcast_to(sink.astype(jnp.float32).reshape(kvh, g)[None, :, :, None, None], s.shape[:-1] + (1,))
    p = jax.nn.softmax(jnp.concatenate([s, sink_col], axis=-1), axis=-1)[..., :-1].astype(v.dtype)
    return jnp.einsum('bhgqk,bkhd->bqhgd', p, v).reshape(bsz, n, -1)


def window_sink_attention(q, k, v, k_ctx, v_ctx, sink, scale):
    bsz, n, kvh, g, d = q.shape
    nblk = n // BLOCK
    pad = ((0, 0), (BLOCK, BLOCK), (0, 0), (0, 0))
    kp = jnp.pad(k, pad).reshape(bsz, nblk + 2, BLOCK, kvh, d)
    vp = jnp.pad(v, pad).reshape(bsz, nblk + 2, BLOCK, kvh, d)
    k_band = jnp.concatenate([kp[:, :-2], kp[:, 1:-1], kp[:, 2:]], axis=2)
    v_band = jnp.concatenate([vp[:, :-2], vp[:, 1:-1], vp[:, 2:]], axis=2)
    qb = q.reshape(bsz, nblk, BLOCK, kvh, g, d)
    s_loc = jnp.einsum('bnqhgd,bnkhd->bnhgqk', qb, k_band).astype(jnp.float32) * scale
    s_ctx = jnp.einsum('bnqhgd,bkhd->bnhgqk', qb, k_ctx).astype(jnp.float32) * scale
    blk = jnp.arange(nblk)[:, None, None] * BLOCK
    q_pos = blk + jnp.arange(BLOCK)[None, :, None]
    k_pos = blk - BLOCK + jnp.arange(3 * BLOCK)[None, None, :]
    valid = (jnp.abs(k_pos - q_pos) <= WINDOW) & (k_pos >= 0) & (k_pos < n)
    s_loc = jnp.where(valid[None, :, None, None], s_loc, -jnp.inf)
    sink_col = jnp.broadcast_to(sink.astype(jnp.float32).reshape(kvh, g)[None, None, :, :, None, None], s_ctx.shape[:-1] + (1,))
    p = jax.nn.softmax(jnp.concatenate([s_ctx, s_loc, sink_col], axis=-1), axis=-1).astype(v.dtype)
    m = k_ctx.shape[1]
    out = (jnp.einsum('bnhgqk,bkhd->bnqhgd', p[..., :m], v_ctx)
           + jnp.einsum('bnhgqk,bnkhd->bnqhgd', p[..., m:m + 3 * BLOCK], v_band))
    return out.reshape(bsz, n, kvh * g * d)


def centred_dwconv(u, w, b):
    k = w.shape[0]
    out = lax.conv_general_dilated(u, w[:, None, :].astype(u.dtype), window_strides=(1,),
                                   padding=[(k // 2, k // 2)], dimension_numbers=('NWC', 'WIO', 'NWC'),
                                   feature_group_count=u.shape[-1])
    return out + b


def ssd_scan(x, dt, bm, cm, a, init, with_output):
    bsz, n, nh, hp = x.shape
    nc = n // SSD_CHUNK
    rep = nh // bm.shape[2]
    bh = jnp.repeat(bm, rep, axis=2).reshape(bsz, nc, SSD_CHUNK, nh, -1)
    xdt = (x * dt[..., None]).reshape(bsz, nc, SSD_CHUNK, nh, hp)
    a_cum = jnp.cumsum((dt * a).reshape(bsz, nc, SSD_CHUNK, nh), axis=2)
    a_tot = a_cum[:, :, -1]
    states = jnp.einsum('bckhn,bckhp->bchpn', bh * jnp.exp(a_tot[:, :, None] - a_cum)[..., None], xdt)

    def step(s, inp):
        st, at = inp
        return s * jnp.exp(at)[:, :, None, None] + st, s

    final, prev = lax.scan(step, init, (jnp.moveaxis(states, 1, 0), jnp.moveaxis(a_tot, 1, 0)))
    if not with_output:
        return None, final
    prev = jnp.moveaxis(prev, 0, 1)
    ch = jnp.repeat(cm, rep, axis=2).reshape(bsz, nc, SSD_CHUNK, nh, -1)
    seg = a_cum[:, :, :, None, :] - a_cum[:, :, None, :, :]
    lower = jnp.tril(jnp.ones((SSD_CHUNK, SSD_CHUNK), dtype=bool))
    decay = jnp.exp(jnp.where(lower[None, None, :, :, None], seg, -jnp.inf))
    scores = jnp.einsum('bcqhn,bckhn->bcqkh', ch, bh) * decay
    y = (jnp.einsum('bcqkh,bckhp->bcqhp', scores, xdt)
         + jnp.einsum('bcqhn,bchpn->bcqhp', ch, prev) * jnp.exp(a_cum)[..., None])
    return y.reshape(bsz, n, nh, hp), final


def mla_keys(ckv, k_rope, kv_norm_w, w_kv_up, rope):
    bsz, n = ckv.shape[:2]
    kv = (rmsnorm(ckv, kv_norm_w) @ w_kv_up).reshape(bsz, n, MLA_HEADS, MLA_NOPE + MLA_V)
    kr = k_rope[:, :, None, :]
    if rope is not None:
        kr = apply_rope(kr, *rope)
    k = jnp.concatenate([kv[..., :MLA_NOPE], jnp.broadcast_to(kr, (bsz, n, MLA_HEADS, MLA_ROPE))], axis=-1)
    return k, kv[..., MLA_NOPE:]


def mla_queries(cq, q_norm_w, w_q_up, rope):
    bsz, n = cq.shape[:2]
    q = (rmsnorm(cq, q_norm_w) @ w_q_up).reshape(bsz, n, MLA_HEADS, MLA_NOPE + MLA_ROPE)
    if rope is not None:
        q = jnp.concatenate([q[..., :MLA_NOPE], apply_rope(q[..., MLA_NOPE:], *rope)], axis=-1)
    return q[:, :, :, None, :]


def mla_branch(lat, ctx, q_norm_w, kv_norm_w, w_q_up, w_kv_up, rope, need_ctx_out):
    cq, ckv, kr, gate = lat
    cq_c, ckv_c, kr_c, gate_c = ctx
    scale = (MLA_NOPE + MLA_ROPE) ** -0.5
    k_l, v_l = mla_keys(ckv, kr, kv_norm_w, w_kv_up, rope)
    k_c, v_c = mla_keys(ckv_c, kr_c, kv_norm_w, w_kv_up, None)
    q_l = mla_queries(cq, q_norm_w, w_q_up, rope)
    out_l = blocked_attention(q_l, jnp.concatenate([k_c, k_l], 1), jnp.concatenate([v_c, v_l], 1), scale) * jax.nn.silu(gate)
    out_c = None
    if need_ctx_out:
        q_c = mla_queries(cq_c, q_norm_w, w_q_up, None)
        out_c = blocked_attention(q_c, k_c, v_c, scale) * jax.nn.silu(gate_c)
    return out_l, out_c


def global_gqa_branch(lat, ctx, q_norm_w, k_norm_w, rope, need_ctx_out):
    q, k, v, gate = lat
    q_c, k_c, v_c, gate_c = ctx
    bsz, n = q.shape[:2]
    m = k_c.shape[1]
    g = GQA_HEADS // GQA_KV_HEADS
    scale = HEAD_DIM ** -0.5
    q = apply_rope(rmsnorm(q.reshape(bsz, n, GQA_HEADS, HEAD_DIM), q_norm_w), *rope).reshape(bsz, n, GQA_KV_HEADS, g, HEAD_DIM)
    k = apply_rope(rmsnorm(k.reshape(bsz, n, GQA_KV_HEADS, HEAD_DIM), k_norm_w), *rope)
    v = v.reshape(bsz, n, GQA_KV_HEADS, HEAD_DIM)
    k_c = rmsnorm(k_c.reshape(bsz, m, GQA_KV_HEADS, HEAD_DIM), k_norm_w)
    v_c = v_c.reshape(bsz, m, GQA_KV_HEADS, HEAD_DIM)
    out_l = blocked_attention(q, jnp.concatenate([k_c, k], 1), jnp.concatenate([v_c, v], 1), scale) * jax.nn.silu(gate)
    out_c = None
    if need_ctx_out:
        q_c = rmsnorm(q_c.reshape(bsz, m, GQA_HEADS, HEAD_DIM), q_norm_w).reshape(bsz, m, GQA_KV_HEADS, g, HEAD_DIM)
        out_c = blocked_attention(q_c, k_c, v_c, scale) * jax.nn.silu(gate_c)
    return out_l, out_c


def window_gqa_branch(lat, ctx, sink, rope, need_ctx_out):
    q, k, v, gate = lat
    q_c, k_c, v_c, gate_c = ctx
    bsz, n = q.shape[:2]
    m = k_c.shape[1]
    g = SWA_HEADS // SWA_KV_HEADS
    scale = HEAD_DIM ** -0.5
    q = apply_rope(q.reshape(bsz, n, SWA_HEADS, HEAD_DIM), *rope).reshape(bsz, n, SWA_KV_HEADS, g, HEAD_DIM)
    k = apply_rope(k.reshape(bsz, n, SWA_KV_HEADS, HEAD_DIM), *rope)
    v = v.reshape(bsz, n, SWA_KV_HEADS, HEAD_DIM)
    k_c = k_c.reshape(bsz, m, SWA_KV_HEADS, HEAD_DIM)
    v_c = v_c.reshape(bsz, m, SWA_KV_HEADS, HEAD_DIM)
    out_l = window_sink_attention(q, k, v, k_c, v_c, sink, scale) * jax.nn.silu(gate)
    out_c = None
    if need_ctx_out:
        q_c = q_c.reshape(bsz, m, SWA_KV_HEADS, g, HEAD_DIM)
        out_c = dense_sink_attention(q_c, k_c, v_c, sink, scale) * jax.nn.silu(gate_c)
    return out_l, out_c


def ssd_prep(xbc, conv_w, conv_b):
    u = jax.nn.silu(centred_dwconv(xbc, conv_w, conv_b))
    xs, bm, cm = jnp.split(u, [SSD_INNER, SSD_INNER + SSD_GROUPS * SSD_STATE], axis=-1)
    bsz, n = xs.shape[:2]
    return (xs.reshape(bsz, n, SSD_HEADS, SSD_HEAD_DIM), bm.reshape(bsz, n, SSD_GROUPS, SSD_STATE),
            cm.reshape(bsz, n, SSD_GROUPS, SSD_STATE))


def ssd_branch(lat, ctx, conv_w, conv_b, a_log, dt_bias, d_skip, norm_w, need_ctx_out):
    z_l, xbc_l, dt_l = lat
    z_c, xbc_c, dt_c = ctx
    x_l, b_l, c_l = ssd_prep(xbc_l, conv_w, conv_b)
    x_c, b_c, c_c = ssd_prep(xbc_c, conv_w, conv_b)
    bsz = x_l.shape[0]
    skip = d_skip.astype(jnp.float32)[None, None, :, None]
    y_l = skip * x_l
    y_c = skip * x_c if need_ctx_out else None
    for direction in range(2):
        a = -jnp.exp(a_log[direction].astype(jnp.float32))
        sl = slice(direction * SSD_HEADS, (direction + 1) * SSD_HEADS)
        bias = dt_bias[direction].astype(jnp.float32)
        seq_l = [x_l, jax.nn.softplus(dt_l[..., sl].astype(jnp.float32) + bias), b_l, c_l]
        seq_c = [x_c, jax.nn.softplus(dt_c[..., sl].astype(jnp.float32) + bias), b_c, c_c]
        if direction == 1:
            seq_l = [jnp.flip(t, axis=1) for t in seq_l]
            seq_c = [jnp.flip(t, axis=1) for t in seq_c]
        init = jnp.zeros((bsz, SSD_HEADS, SSD_HEAD_DIM, SSD_STATE), jnp.float32)
        yc_dir, s_ctx = ssd_scan(*seq_c, a, init, need_ctx_out)
        yl_dir, _ = ssd_scan(*seq_l, a, s_ctx, True)
        if direction == 1:
            yl_dir = jnp.flip(yl_dir, axis=1)
            yc_dir = jnp.flip(yc_dir, axis=1) if need_ctx_out else None
        y_l = y_l + yl_dir
        if need_ctx_out:
            y_c = y_c + yc_dir
    n = x_l.shape[1]
    out_l = rmsnorm(y_l.reshape(bsz, n, SSD_INNER) * jax.nn.silu(z_l), norm_w).astype(z_l.dtype)
    out_c = None
    if need_ctx_out:
        m = x_c.shape[1]
        out_c = rmsnorm(y_c.reshape(bsz, m, SSD_INNER) * jax.nn.silu(z_c), norm_w).astype(z_c.dtype)
    return out_l, out_c


def hybrid_layer(x, xc, c, c_ctx, norm_w, w_mod, b_mod, w_in, mla_q_norm, mla_kv_norm, mla_w_q_up, mla_w_kv_up,
                 gqa_q_norm, gqa_k_norm, swa_sink, ssd_conv_w, ssd_conv_b, ssd_a_log, ssd_dt_bias, ssd_d,
                 ssd_norm_w, w_out, rope_mla, rope_head, need_ctx_out):
    shift, scale, gate = jnp.split(jax.nn.silu(c) @ w_mod + b_mod, 3, axis=-1)
    shift_c, scale_c, gate_c = jnp.split(jax.nn.silu(c_ctx) @ w_mod + b_mod, 3, axis=-1)
    h = rmsnorm(x, norm_w) * (1 + scale[:, None]) + shift[:, None]
    hc = rmsnorm(xc, norm_w) * (1 + scale_c) + shift_c
    pl = split_cols(h @ w_in, IN_WIDTHS)
    pc = split_cols(hc @ w_in, IN_WIDTHS)
    ya_l, ya_c = mla_branch(pl[0:4], pc[0:4], mla_q_norm, mla_kv_norm, mla_w_q_up, mla_w_kv_up, rope_mla, need_ctx_out)
    yb_l, yb_c = global_gqa_branch(pl[4:8], pc[4:8], gqa_q_norm, gqa_k_norm, rope_head, need_ctx_out)
    yc_l, yc_c = window_gqa_branch(pl[8:12], pc[8:12], swa_sink, rope_head, need_ctx_out)
    yd_l, yd_c = ssd_branch(pl[12:15], pc[12:15], ssd_conv_w, ssd_conv_b, ssd_a_log, ssd_dt_bias, ssd_d,
                            ssd_norm_w, need_ctx_out)
    x = x + gate[:, None] * (jnp.concatenate([ya_l, yb_l, yc_l, yd_l], axis=-1) @ w_out)
    if need_ctx_out:
        xc = xc + gate_c * (jnp.concatenate([ya_c, yb_c, yc_c, yd_c], axis=-1) @ w_out)
    return x, xc


def setup_inputs(seed: int = 0) -> dict:
    key = jax.random.key(seed)
    ks = jax.random.split(key, 24)
    f32 = jnp.float32

    def nrm(k, shape, s):
        return jax.random.normal(k, shape, f32) * s

    dt_init = jnp.exp(jax.random.uniform(ks[18], (DEPTH, 2, SSD_HEADS), f32, jnp.log(1e-3), jnp.log(1e-1)))
    return {
        'x': nrm(ks[0], (BATCH, SEQ, D_MODEL), 1.0),
        'c': nrm(ks[1], (BATCH, D_MODEL), 1.0),
        'ctx': nrm(ks[2], (BATCH, CTX_LEN, D_MODEL), 1.0),
        'c_ctx': nrm(ks[3], (D_MODEL,), 1.0),
        'norm_w': 1.0 + nrm(ks[4], (DEPTH, D_MODEL), 0.02),
        'w_mod': nrm(ks[5], (DEPTH, D_MODEL, 3 * D_MODEL), 0.5 * D_MODEL ** -0.5),
        'b_mod': nrm(ks[6], (DEPTH, 3 * D_MODEL), 0.01),
        'w_in': nrm(ks[7], (DEPTH, D_MODEL, IN_COLS), D_MODEL ** -0.5),
        'mla_q_norm': 1.0 + nrm(ks[8], (DEPTH, MLA_Q_RANK), 0.02),
        'mla_kv_norm': 1.0 + nrm(ks[9], (DEPTH, MLA_KV_RANK), 0.02),
        'mla_w_q_up': nrm(ks[10], (DEPTH, MLA_Q_RANK, MLA_HEADS * (MLA_NOPE + MLA_ROPE)), MLA_Q_RANK ** -0.5),
        'mla_w_kv_up': nrm(ks[11], (DEPTH, MLA_KV_RANK, MLA_HEADS * (MLA_NOPE + MLA_V)), MLA_KV_RANK ** -0.5),
        'gqa_q_norm': 1.0 + nrm(ks[12], (DEPTH, HEAD_DIM), 0.02),
        'gqa_k_norm': 1.0 + nrm(ks[13], (DEPTH, HEAD_DIM), 0.02),
        'swa_sink': nrm(ks[14], (DEPTH, SWA_HEADS), 0.5),
        'ssd_conv_w': nrm(ks[15], (DEPTH, SSD_CONV, SSD_CONV_DIM), SSD_CONV ** -0.5),
        'ssd_conv_b': nrm(ks[16], (DEPTH, SSD_CONV_DIM), 0.01),
        'ssd_a_log': jnp.log(jax.random.uniform(ks[17], (DEPTH, 2, SSD_HEADS), f32, 1.0, 16.0)),
        'ssd_dt_bias': dt_init + jnp.log(-jnp.expm1(-dt_init)),
        'ssd_d': 1.0 + nrm(ks[19], (DEPTH, SSD_HEADS), 0.02),
        'ssd_norm_w': 1.0 + nrm(ks[20], (DEPTH, SSD_INNER), 0.02),
        'w_out': nrm(ks[21], (DEPTH, D_MIX, D_MODEL), D_MIX ** -0.5),
        'final_norm_w': 1.0 + nrm(ks[22], (D_MODEL,), 0.02),
    }


def reference(x, c, ctx, c_ctx, norm_w, w_mod, b_mod, w_in, mla_q_norm, mla_kv_norm, mla_w_q_up, mla_w_kv_up,
              gqa_q_norm, gqa_k_norm, swa_sink, ssd_conv_w, ssd_conv_b, ssd_a_log, ssd_dt_bias, ssd_d,
              ssd_norm_w, w_out, final_norm_w):
    n = x.shape[1]
    rope_head = axial_rope(n, HEAD_DIM)
    rope_mla = axial_rope(n, MLA_ROPE)
    xc = ctx
    for i in range(DEPTH):
        x, xc = hybrid_layer(x, xc, c, c_ctx, norm_w[i], w_mod[i], b_mod[i], w_in[i], mla_q_norm[i], mla_kv_norm[i],
                             mla_w_q_up[i], mla_w_kv_up[i], gqa_q_norm[i], gqa_k_norm[i], swa_sink[i],
                             ssd_conv_w[i], ssd_conv_b[i], ssd_a_log[i], ssd_dt_bias[i], ssd_d[i], ssd_norm_w[i],
                             w_out[i], rope_mla, rope_head, i < DEPTH - 1)
    return rmsnorm(x, final_norm_w)
```

```python
import contextlib
import numpy as np
import concourse.bass as bass
import concourse.mybir as mybir
from concourse.bass_utils import run_bass_kernel_spmd

F32 = mybir.dt.float32
BF16 = mybir.dt.bfloat16
AF = mybir.ActivationFunctionType
ALU = mybir.AluOpType

L = 4
D = 1024
SQ = "act"
NCTX = 256
NLAT = 4096
TT = NCTX + NLAT
NT = TT // 128
EPS = 1e-6
GROUPS = [(0, 256)] + [(256 + 512 * i, 512) for i in range(8)]
NBLK = 34
NCOLS = NBLK * 128
NV = 56


class Res:
    __slots__ = ("name", "w", "r", "t", "psum")

    def __init__(self, name, t=None, psum=False):
        self.name = name
        self.w = None
        self.r = []
        self.t = t
        self.psum = psum

    def __getitem__(self, k):
        return self.t[k]


class FW:
    EPOCH = 30000

    def __init__(self, nc, es):
        self.nc = nc
        self.es = es
        self.eng = {"pe": nc.tensor, "act": nc.scalar, "dve": nc.vector, "pool": nc.gpsimd, "sp": nc.sync}
        self.esem = {}
        self.ecnt = {}
        self.own = {k: set() for k in self.eng}
        self.waited = {k: {} for k in self.eng}
        self.dsems = {}
        self.free_dsems = []
        self.nsem = 0
        self.stopped = False
        self.uid = 0
        self.ninstr = 0
        for k in ("pe", "act", "dve", "pool"):
            self._new_epoch(k)

    def _sem(self, name):
        self.nsem += 1
        return self.es.enter_context(self.nc.semaphore(f"{name}_{self.nsem}"))

    def _new_epoch(self, k):
        self.esem[k] = self._sem("e" + k)
        self.own[k].add(id(self.esem[k]))
        self.ecnt[k] = 0

    def sb(self, st, name, shape, dt):
        self.uid += 1
        return Res(name, st.enter_context(self.nc.sbuf_tensor(f"sb_{name}_{self.uid}", list(shape), dt)))

    def ps(self, st, name, shape, dt=F32):
        self.uid += 1
        return Res(name, st.enter_context(self.nc.psum_tensor(f"ps_{name}_{self.uid}", list(shape), dt)), psum=True)

    def _wait(self, e, tok):
        if tok is None:
            return
        sem, val = tok
        key = id(sem)
        if e == "pe" and key in self.own["pe"]:
            return
        prev = self.waited[e].get(key)
        if prev is not None and prev >= val:
            return
        self.eng[e].wait_ge(sem, val)
        self.waited[e][key] = val

    def _deps(self, e, reads, writes):
        for r in reads:
            self._wait(e, r.w)
            if r.psum:
                for tok in r.r:
                    self._wait(e, tok)
        for w in writes:
            self._wait(e, w.w)
            for tok in w.r:
                self._wait(e, tok)

    def _commit(self, tok, reads, writes):
        for r in reads:
            if r in writes:
                continue
            r.r.append(tok)
            if len(r.r) > 16:
                d = {}
                for s, v in r.r:
                    k = id(s)
                    if k not in d or d[k][1] < v:
                        d[k] = (s, v)
                r.r = list(d.values())
        for w in writes:
            w.w = tok
            w.r = []

    def op(self, e, reads, writes, fn):
        if self.stopped:
            return None
        self._deps(e, reads, writes)
        ins = fn()
        if self.ecnt[e] >= self.EPOCH:
            self._new_epoch(e)
        self.ecnt[e] += 1
        tok = (self.esem[e], self.ecnt[e])
        ins.then_inc(tok[0], 1)
        self._commit(tok, reads, writes)
        self.ninstr += 1
        return ins

    def dma(self, q, out_ap, in_ap, reads, writes, key, **kw):
        if self.stopped:
            return None
        self._deps(q, reads, writes)
        if key not in self.dsems:
            self.dsems[key] = self.free_dsems.pop() if self.free_dsems else [self._sem("d"), 0]
        ent = self.dsems[key]
        ins = self.eng[q].dma_start(out=out_ap, in_=in_ap, **kw)
        ent[1] += 16
        tok = (ent[0], ent[1])
        ins.then_inc(tok[0], 16)
        self._commit(tok, reads, writes)
        self.ninstr += 1
        return tok

    def ld(self, res, sb_ap, dram_ap, q="sp", extra_reads=(), **kw):
        return self.dma(q, sb_ap, dram_ap, list(extra_reads), [res], "L" + res.name, **kw)

    def st(self, dram_ap, res, sb_ap, q="sp", extra_writes=(), **kw):
        return self.dma(q, dram_ap, sb_ap, [res], list(extra_writes), "S" + res.name, **kw)

    def barrier(self, force=False):
        if self.stopped and not force:
            return
        toks = [(self.esem[k], self.ecnt[k]) for k in ("pe", "act", "dve", "pool") if self.ecnt[k] > 0]
        toks += [(s, c) for s, c in self.dsems.values() if c > 0]
        for e in ("pe", "act", "dve", "pool", "sp"):
            for t in toks:
                if e == "pe" and id(t[0]) in self.own["pe"]:
                    continue
                self._wait(e, t)
        self.free_dsems.extend(self.dsems.values())
        self.dsems = {}


class _Stop(Exception):
    pass


def run_window(gen_iter, width):
    it = iter(gen_iter)

    def start(idx):
        f = next(it, None)
        if f is None:
            return None
        return f(idx) if callable(f) else f

    active = [start(i) for i in range(width)]
    while any(g_ is not None for g_ in active):
        for idx in range(width):
            g_ = active[idx]
            if g_ is None:
                continue
            try:
                next(g_)
            except StopIteration:
                active[idx] = start(idx)


def run_chains(gens):
    gens = list(gens)
    while gens:
        for g_ in list(gens):
            try:
                next(g_)
            except StopIteration:
                gens.remove(g_)


def build(nlayers=L, dbg=None, stop=None):
    nc = bass.Bass("TRN2", target_bir_lowering=False)
    V, A, G, T = nc.vector, nc.scalar, nc.gpsimd, nc.tensor

    def din(name, shape, dt=F32):
        return nc.dram_tensor(name, list(shape), dt, kind="ExternalInput").ap()

    def dscr(name, shape, dt):
        kind = "ExternalOutput" if (dbg and name in dbg) else "Internal"
        return nc.dram_tensor(name, list(shape), dt, kind=kind).ap()

    x_in = din("x", [NLAT, D])
    ctx_in = din("ctx", [NCTX, D])
    ccT_in = din("ccT", [128, 8, 2])
    w_mod = din("w_mod", [L, D, 3 * D])
    b_mod = din("b_mod", [L, 3 * D])
    norm_w = din("norm_w", [L, D])
    w_in_p = din("w_in_p", [L, D, NCOLS])
    wq_up_p = din("wq_up_p", [L, 256, 2, 384])
    wkv_k = din("wkv_k", [L, 128, 256])
    wkv_v = din("wkv_v", [L, 128, 256])
    colv_in = din("colv", [L, 128, NV])
    alog_in = din("a_log", [L, 8])
    dtb_in = din("dt_bias", [L, 8])
    sink_in = din("sink", [L, 4])
    w_out = din("w_out", [L, D, D])
    fnw_in = din("final_norm_w", [D])
    consts_in = din("consts", [128, 5, 128])
    cos64_in = din("cos64", [128, TT])
    sin64_in = din("sin64", [128, TT])
    cos96_in = din("cos96", [128, TT])
    sin96_in = din("sin96", [128, TT])
    out_d = nc.dram_tensor("out", [NLAT, D], F32, kind="ExternalOutput").ap()

    mod_d = dscr("mod_d", [L, 2, 3 * D], F32)
    xres_d = dscr("xres_d", [TT, D], F32)
    qA_d = dscr("qA_d", [4, 96, TT], BF16)
    kA_d = dscr("kA_d", [4, 64, TT], BF16)
    kr_d = dscr("kr_d", [32, TT], BF16)
    VA_d = dscr("VA_d", [TT, 256], BF16)
    gA_d = dscr("gA_d", [256, TT], BF16)
    qB_d = dscr("qB_d", [2, 128, TT], BF16)
    kB_d = dscr("kB_d", [128, TT], BF16)
    VB_d = dscr("VB_d", [TT, 128], BF16)
    gB_d = dscr("gB_d", [256, TT], BF16)
    qC_d = dscr("qC_d", [2, 128, TT], BF16)
    kC_d = dscr("kC_d", [128, TT], BF16)
    VC_d = dscr("VC_d", [TT, 128], BF16)
    gC_d = dscr("gC_d", [256, TT], BF16)
    zs_d = dscr("zs_d", [256, TT], BF16)
    xbc_d = dscr("xbc_d", [768, TT], F32)
    dt_d = dscr("dt_d", [TT, 8], F32)
    ycat_d = dscr("ycat_d", [NT, 128, 8, 128], BF16)

    def ycat_rows(r0, nrows, t0, wd):
        k, p0 = r0 // 128, r0 % 128
        return ycat_d[t0 // 128:(t0 + wd) // 128, p0:p0 + nrows, k, :].rearrange("t p c -> p t c")

    dbgu_d = dscr("dbgu_d", [768, TT], BF16)
    dbgy_d = dscr("dbgy_d", [256, TT], F32)

    with contextlib.ExitStack() as es:
        fw = FW(nc, es)
        op, ld, st = fw.op, fw.ld, fw.st

        consts = fw.sb(es, "consts", [128, 5, 128], F32)
        cbf = fw.sb(es, "cbf", [128, 5, 128], BF16)
        epsb = fw.sb(es, "epsb", [128, 1], F32)
        ld(consts, consts[:], consts_in[:, :, :])
        op("dve", [consts], [cbf], lambda: V.tensor_copy(cbf[:], consts[:]))
        op("pool", [], [epsb], lambda: G.memset(epsb[:], EPS))
        P = [fw.ps(es, f"P{i}", [128, 512], F32) for i in range(8)]

        def rstd_from(ps_res, ps_ap, out_res, out_ap, n):
            op("act", [ps_res, epsb], [out_res], lambda: A.activation(out_ap, ps_ap, AF.Ln, bias=epsb[0:ps_ap.shape[0], 0:1], scale=1.0 / n))
            op("act", [out_res], [out_res], lambda: A.activation(out_ap, out_ap, AF.Exp, scale=-0.5))

        with contextlib.ExitStack() as s0:
            cc = fw.sb(s0, "cc", [128, 8, 2], F32)
            ce = fw.sb(s0, "ce", [128, 8, 2], F32)
            sc = fw.sb(s0, "sc", [128, 8, 2], F32)
            wm = [fw.sb(s0, f"wm{i}", [128, 8, 512], F32) for i in range(2)]
            bm = [fw.sb(s0, f"bm{i}", [2, 512], F32) for i in range(2)]
            mo = [fw.sb(s0, f"mo{i}", [2, 512], F32) for i in range(2)]
            ld(cc, cc[:], ccT_in[:, :, :])
            op("act", [cc], [ce], lambda: A.activation(ce[:], cc[:], AF.Exp, scale=-1.0))
            op("dve", [ce], [ce], lambda: V.tensor_scalar_add(ce[:], ce[:], 1.0))
            op("dve", [ce], [ce], lambda: V.reciprocal(ce[:], ce[:]))
            op("dve", [cc, ce], [sc], lambda: V.tensor_tensor(sc[:], cc[:], ce[:], ALU.mult))
            it = 0
            for l in range(nlayers):
                for cg in range(6):
                    b = it % 2
                    it += 1
                    ld(wm[b], wm[b][:], w_mod[l, :, cg * 512:(cg + 1) * 512].rearrange("(k p) c -> p k c", p=128), q=("sp" if b == 0 else "act"))
                    ld(bm[b], bm[b][:], b_mod[l, cg * 512:(cg + 1) * 512].partition_broadcast(2))
                    pm = P[b]
                    for k in range(8):
                        op("pe", [sc, wm[b]], [pm], lambda k=k, b=b, pm=pm: T.matmul(pm[0:2, :], sc[:, k, :], wm[b][:, k, :], start=(k == 0), stop=(k == 7)))
                    op("dve", [pm, bm[b]], [mo[b]], lambda b=b, pm=pm: V.tensor_tensor(mo[b][:], pm[0:2, :], bm[b][:], ALU.add))
                    st(mod_d[l, :, cg * 512:(cg + 1) * 512], mo[b], mo[b][:])
            fw.barrier()
        def chk(tag):
            if stop == tag:
                fw.stopped = True
        try:
          chk("S0")
          for l in range(nlayers):
              last = (l == L - 1)
              with contextlib.ExitStack() as sl:
                  colv = fw.sb(sl, "colv", [128, NV], F32)
                  ld(colv, colv[:], colv_in[l, :, :])
                  with contextlib.ExitStack() as s12:
                      hT = fw.sb(s12, "hT", [128, 8, TT], BF16)
                      with contextlib.ExitStack() as s1:
                          gmul = [fw.sb(s1, f"gmul{j}", [128, D], F32) for j in range(2)]
                          shf = [fw.sb(s1, f"shf{j}", [128, D], F32) for j in range(2)]
                          nwb = fw.sb(s1, "nwb", [128, D], F32)
                          tb = fw.sb(s1, "tb", [128, D], F32)
                          xt = [fw.sb(s1, f"xt{i}", [128, D], F32) for i in range(3)]
                          junk = fw.sb(s1, "junk", [128, D], BF16)
                          h1 = [fw.sb(s1, f"h1{i}", [128, D], F32) for i in range(3)]
                          hb = [fw.sb(s1, f"hb{i}", [128, D], BF16) for i in range(3)]
                          ss = [fw.sb(s1, f"ss{i}", [128, 1], F32) for i in range(3)]
                          ld(nwb, nwb[:], norm_w[l, :].partition_broadcast(128))
                          for j in range(2):
                              ld(tb, tb[:], mod_d[l, j, D:2 * D].partition_broadcast(128))
                              op("dve", [tb, nwb], [gmul[j]], lambda j=j: V.scalar_tensor_tensor(gmul[j][:], tb[:], 1.0, nwb[:], ALU.add, ALU.mult))
                              ld(shf[j], shf[j][:], mod_d[l, j, 0:D].partition_broadcast(128))
                          def s1_tile(t):
                              j = 1 if t < 2 else 0
                              if l == 0:
                                  src = ctx_in[t * 128:(t + 1) * 128, :] if t < 2 else x_in[(t - 2) * 128:(t - 1) * 128, :]
                              else:
                                  src = xres_d[t * 128:(t + 1) * 128, :]
                              X, S_, H1, HB, pt = xt[t % 3], ss[t % 3], h1[t % 3], hb[t % 3], P[t % 3]
                              ld(X, X[:], src)
                              op("pool", [], [S_], lambda: G.memset(S_[:], 0.0))
                              yield
                              op("act", [X], [junk, S_], lambda: A.activation(junk[:], X[:], AF.Square, accum_out=S_[:, 0:1]))
                              yield
                              op("act", [S_, epsb], [S_], lambda: A.activation(S_[:, 0:1], S_[:, 0:1], AF.Ln, bias=epsb[:, 0:1], scale=1.0 / D))
                              yield
                              op("act", [S_], [S_], lambda: A.activation(S_[:, 0:1], S_[:, 0:1], AF.Exp, scale=-0.5))
                              yield
                              op("dve", [X, S_, gmul[j]], [H1], lambda: V.scalar_tensor_tensor(H1[:], X[:], S_[:, 0:1], gmul[j][:], ALU.mult, ALU.mult))
                              yield
                              op("dve", [H1, shf[j]], [HB], lambda: V.tensor_tensor(HB[:], H1[:], shf[j][:], ALU.add))
                              yield
                              ptb = pt.t.bitcast(BF16)
                              for k in range(8):
                                  op("pe", [HB, cbf], [pt], lambda k=k: T.transpose(ptb[:, k * 128:(k + 1) * 128], HB[:, k * 128:(k + 1) * 128], cbf[:, 0, :]))
                              yield
                              op("act", [pt], [hT], lambda: A.copy(hT[:, :, t * 128:(t + 1) * 128], ptb[:, 0:1024].rearrange("p (k c) -> p k c", k=8)))
                              yield

                          run_window((s1_tile(t) for t in range(NT)), 3)
                          fw.barrier()
                      chk("S1")

                      with contextlib.ExitStack() as s2:
                          wst = [fw.sb(s2, f"wst{i}", [128, 8, 128], F32) for i in range(2)]
                          wb = [fw.sb(s2, f"wb{i}", [128, 8, 128], BF16) for i in range(8)]
                          wqs = fw.sb(s2, "wqs", [128, 2, 768], F32)
                          wq = fw.sb(s2, "wq", [128, 2, 768], BF16)
                          wks = fw.sb(s2, "wks", [128, 512], F32)
                          wk = fw.sb(s2, "wk", [128, 512], BF16)
                          tab = [[fw.sb(s2, f"tab{i}_{j}", [128, 512], F32) for j in range(2)] for i in range(3)]
                          sq_ = [fw.sb(s2, f"sq{i}", [128, 2, 512], BF16) for i in range(2)]
                          rs_ = [fw.sb(s2, f"rs{i}", [128, 512], F32) for i in range(2)]
                          cqn_ = [fw.sb(s2, f"cqn{i}", [128, 2, 512], BF16) for i in range(2)]
                          t1 = [fw.sb(s2, f"t1{i}", [128, 512], F32) for i in range(2)]
                          t2 = [fw.sb(s2, f"t2{i}", [128, 512], F32) for i in range(2)]
                          t3 = [fw.sb(s2, f"t3{i}", [128, 512], F32) for i in range(2)]
                          ob = [fw.sb(s2, f"ob{i}", [128, 512], BF16) for i in range(4)]
                          of = [fw.sb(s2, f"of{i}", [128, 512], F32) for i in range(2)]
                          cnt = {"wst": 0, "wb": 0, "ob": 0, "t": 0, "of": 0, "p": 0, "tab": 0}

                          def nxt(key, n):
                              v = cnt[key] % n
                              cnt[key] += 1
                              return v

                          def load_blocks(blks):
                              res = []
                              for bk in blks:
                                  ws = wst[nxt("wst", 2)]
                                  w = wb[nxt("wb", 8)]
                                  ld(ws, ws[:], w_in_p[l, :, bk * 128:(bk + 1) * 128].rearrange("(k p) c -> p k c", p=128))
                                  op("act", [ws], [w], lambda ws=ws, w=w: A.copy(w[:], ws[:]))
                                  res.append(w)
                              return res

                          def proj(w, t0, wd, pr):
                              for k in range(8):
                                  op("pe", [hT, w], [pr], lambda k=k: T.matmul(pr[:, 0:wd], w[:, k, :], hT[:, k, t0:t0 + wd], start=(k == 0), stop=(k == 7)))

                          def nextP():
                              return P[nxt("p", 8)]

                          def load_tabs(t0, wd, c_in, s_in):
                              tb_ = tab[nxt("tab", 3)]
                              ld(tb_[0], tb_[0][:, 0:wd], c_in[:, t0:t0 + wd])
                              ld(tb_[1], tb_[1][:, 0:wd], s_in[:, t0:t0 + wd])
                              return tb_

                          def rope_g(pa, pb, tb_, r0, r1, wd, wcol, wcolp, rstd, dst_ap):
                              i = nxt("t", 2)
                              a1, a2, a3 = t1[i], t2[i], t3[i]
                              o = ob[nxt("ob", 4)]
                              if wcol is None:
                                  op("dve", [pa, tb_[0]], [a1], lambda: V.tensor_tensor(a1[r0:r1, 0:wd], pa[r0:r1, 0:wd], tb_[0][r0:r1, 0:wd], ALU.mult))
                                  op("dve", [pb, tb_[1]], [a2], lambda: V.tensor_tensor(a2[r0:r1, 0:wd], pb[r0:r1, 0:wd], tb_[1][r0:r1, 0:wd], ALU.mult))
                              else:
                                  op("dve", [pa, tb_[0], colv], [a1], lambda: V.scalar_tensor_tensor(a1[r0:r1, 0:wd], pa[r0:r1, 0:wd], colv[r0:r1, wcol:wcol + 1], tb_[0][r0:r1, 0:wd], ALU.mult, ALU.mult))
                                  op("dve", [pb, tb_[1], colv], [a2], lambda: V.scalar_tensor_tensor(a2[r0:r1, 0:wd], pb[r0:r1, 0:wd], colv[r0:r1, wcolp:wcolp + 1], tb_[1][r0:r1, 0:wd], ALU.mult, ALU.mult))
                              yield
                              if rstd is None:
                                  op("pool", [a1, a2], [o], lambda: G.tensor_tensor(o[r0:r1, 0:wd], a1[r0:r1, 0:wd], a2[r0:r1, 0:wd], ALU.add))
                              else:
                                  op("pool", [a1, a2], [a3], lambda: G.tensor_tensor(a3[r0:r1, 0:wd], a1[r0:r1, 0:wd], a2[r0:r1, 0:wd], ALU.add))
                                  yield
                                  op("dve", [a3, rstd], [o], lambda: V.tensor_tensor(o[r0:r1, 0:wd], a3[r0:r1, 0:wd], rstd[r0:r1, 0:wd], ALU.mult))
                              st(dst_ap, o, o[r0:r1, 0:wd], q=SQ)
                              yield

                          def sumsq_g(srcs, ones_idx, n, wd, slot):
                              sq, rs = sq_[slot], rs_[slot]
                              for j, pr in enumerate(srcs):
                                  op("act", [pr], [sq], lambda j=j, pr=pr: A.activation(sq[:, j, 0:wd], pr[:, 0:wd], AF.Square))
                              yield
                              pss = nextP()
                              for j in range(len(srcs)):
                                  op("pe", [sq, cbf], [pss], lambda j=j: T.matmul(pss[:, 0:wd], cbf[:, ones_idx, :], sq[:, j, 0:wd], start=(j == 0), stop=(j == len(srcs) - 1)))
                              yield
                              op("act", [pss, epsb], [rs], lambda: A.activation(rs[:, 0:wd], pss[:, 0:wd], AF.Ln, bias=epsb[:, 0:1], scale=1.0 / n))
                              yield
                              op("act", [rs], [rs], lambda: A.activation(rs[:, 0:wd], rs[:, 0:wd], AF.Exp, scale=-0.5))
                              yield

                          ld(wqs, wqs[:], wq_up_p[l, :, :, :].rearrange("(k p) v c -> p k (v c)", p=128))
                          op("pool", [wqs], [wq], lambda: G.tensor_copy(wq[:], wqs[:]))
                          ld(wks, wks[:, 0:256], wkv_k[l, :, :])
                          ld(wks, wks[:, 256:512], wkv_v[l, :, :])
                          op("pool", [wks], [wk], lambda: G.tensor_copy(wk[:], wks[:]))
                          wcq = load_blocks([0, 1])

                          def cq_group(t0, wd, slot):
                              cqn, rs = cqn_[slot], rs_[slot]
                              tb_ = load_tabs(t0, wd, cos96_in, sin96_in)
                              pa, pb = nextP(), nextP()
                              proj(wcq[0], t0, wd, pa)
                              proj(wcq[1], t0, wd, pb)
                              yield
                              yield from sumsq_g([pa, pb], 3, 256, wd, slot)
                              for j, pr in enumerate((pa, pb)):
                                  op("dve", [pr, colv, rs], [cqn], lambda j=j, pr=pr: V.scalar_tensor_tensor(cqn[:, j, 0:wd], pr[:, 0:wd], colv[:, 4 + j:5 + j], rs[:, 0:wd], ALU.mult, ALU.mult))
                              yield
                              for h in range(4):
                                  po, pp = nextP(), nextP()
                                  for v, pr in enumerate((po, pp)):
                                      for k in range(2):
                                          op("pe", [wq, cqn], [pr], lambda v=v, k=k, pr=pr, h=h: T.matmul(pr[0:96, 0:wd], wq[:, k, v * 384 + h * 96: v * 384 + (h + 1) * 96], cqn[:, k, 0:wd], start=(k == 0), stop=(k == 1)))
                                  yield
                                  yield from rope_g(po, pp, tb_, 0, 96, wd, None, None, None, qA_d[h, :, t0:t0 + wd])

                          run_window((cq_group(t0, wd, gi % 2) for gi, (t0, wd) in enumerate(GROUPS)), 2)
                          chk("S2.q")
                          wckv = load_blocks([2, 3, 4])

                          def ckv_group(t0, wd, slot):
                              cqn, rs = cqn_[slot], rs_[slot]
                              tb_ = load_tabs(t0, wd, cos96_in, sin96_in)
                              pa = nextP()
                              proj(wckv[0], t0, wd, pa)
                              pra, prb = nextP(), nextP()
                              proj(wckv[1], t0, wd, pra)
                              proj(wckv[2], t0, wd, prb)
                              yield
                              yield from sumsq_g([pa], 3, 128, wd, slot)
                              op("dve", [pa, colv, rs], [cqn], lambda: V.scalar_tensor_tensor(cqn[:, 0, 0:wd], pa[:, 0:wd], colv[:, 6:7], rs[:, 0:wd], ALU.mult, ALU.mult))
                              yield
                              yield from rope_g(pra, prb, tb_, 64, 96, wd, None, None, None, kr_d[:, t0:t0 + wd])
                              for h in range(4):
                                  pk = nextP()
                                  op("pe", [wk, cqn], [pk], lambda pk=pk, h=h: T.matmul(pk[0:64, 0:wd], wk[:, h * 64:(h + 1) * 64], cqn[:, 0, 0:wd], start=True, stop=True))
                                  yield
                                  o = ob[nxt("ob", 4)]
                                  op("act", [pk], [o], lambda pk=pk, o=o: A.copy(o[0:64, 0:wd], pk[0:64, 0:wd]))
                                  st(kA_d[h, :, t0:t0 + wd], o, o[0:64, 0:wd], q=SQ)
                                  yield
                              for tt in range(wd // 128):
                                  pv = nextP()
                                  op("pe", [wk, cqn], [pv], lambda pv=pv, tt=tt: T.matmul(pv[:, 0:256], cqn[:, 0, tt * 128:(tt + 1) * 128], wk[:, 256:512], start=True, stop=True))
                                  yield
                                  o = ob[nxt("ob", 4)]
                                  op("act", [pv], [o], lambda pv=pv, o=o: A.copy(o[:, 0:256], pv[:, 0:256]))
                                  st(VA_d[t0 + tt * 128:t0 + (tt + 1) * 128, :], o, o[:, 0:256], q=SQ)
                                  yield

                          run_window((ckv_group(t0, wd, gi % 2) for gi, (t0, wd) in enumerate(GROUPS)), 2)

                          chk("S2.kv")
                          def gate_unit(blks, dst):
                              ws = load_blocks(blks)
                              for (t0, wd) in GROUPS:
                                  for j, w in enumerate(ws):
                                      pa = nextP()
                                      proj(w, t0, wd, pa)
                                      o = ob[nxt("ob", 4)]
                                      op("act", [pa], [o], lambda pa=pa, o=o: A.activation(o[:, 0:wd], pa[:, 0:wd], AF.Silu))
                                      st(dst[j * 128:(j + 1) * 128, t0:t0 + wd], o, o[:, 0:wd], q=SQ)

                          gate_unit([5, 6], gA_d)
                          chk("S2.gA")

                          def qk_unit(blk_o, blk_p, dsts, norm_cols):
                              ws = load_blocks(list(blk_o) + list(blk_p))
                              n = len(blk_o)

                              def qk_group(t0, wd, slot):
                                  tb_ = load_tabs(t0, wd, cos64_in, sin64_in)
                                  for j in range(n):
                                      pa, pb = nextP(), nextP()
                                      proj(ws[j], t0, wd, pa)
                                      proj(ws[n + j], t0, wd, pb)
                                      yield
                                      if norm_cols is not None:
                                          yield from sumsq_g([pa], 4, 64, wd, slot)
                                          yield from rope_g(pa, pb, tb_, 0, 128, wd, norm_cols[0], norm_cols[1], rs_[slot], dsts[j][:, t0:t0 + wd])
                                      else:
                                          yield from rope_g(pa, pb, tb_, 0, 128, wd, None, None, None, dsts[j][:, t0:t0 + wd])

                              run_window((qk_group(t0, wd, gi % 2) for gi, (t0, wd) in enumerate(GROUPS)), 2)

                          def v_unit(blk, dst):
                              (w,) = load_blocks([blk])
                              for t in range(NT):
                                  pv = nextP()
                                  for k in range(8):
                                      op("pe", [hT, w], [pv], lambda k=k, pv=pv, t=t: T.matmul(pv[:, 0:128], hT[:, k, t * 128:(t + 1) * 128], w[:, k, :], start=(k == 0), stop=(k == 7)))
                                  o = ob[nxt("ob", 4)]
                                  op("act", [pv], [o], lambda pv=pv, o=o: A.copy(o[:, 0:128], pv[:, 0:128]))
                                  st(dst[t * 128:(t + 1) * 128, :], o, o[:, 0:128], q=SQ)

                          qk_unit([7, 8], [9, 10], [qB_d[0], qB_d[1]], (0, 1))
                          chk("S2.qB")
                          qk_unit([11], [12], [kB_d], (2, 3))
                          chk("S2.kB")
                          v_unit(13, VB_d)
                          chk("S2.vB")
                          gate_unit([14, 15], gB_d)
                          qk_unit([16, 17], [18, 19], [qC_d[0], qC_d[1]], None)
                          qk_unit([20], [21], [kC_d], None)
                          v_unit(22, VC_d)
                          gate_unit([23, 24], gC_d)
                          gate_unit([25, 26], zs_d)
                          chk("S2.z")
                          for half in range(2):
                              ws = load_blocks([27 + 3 * half + i for i in range(3)])
                              for (t0, wd) in GROUPS:
                                  for j, w in enumerate(ws):
                                      pa = nextP()
                                      proj(w, t0, wd, pa)
                                      o = of[nxt("of", 2)]
                                      op("act", [pa], [o], lambda pa=pa, o=o: A.copy(o[:, 0:wd], pa[:, 0:wd]))
                                      r0 = (3 * half + j) * 128
                                      st(xbc_d[r0:r0 + 128, t0:t0 + wd], o, o[:, 0:wd], q=SQ)
                          (w,) = load_blocks([33])
                          for t in range(NT):
                              pv = nextP()
                              for k in range(8):
                                  op("pe", [hT, w], [pv], lambda k=k, pv=pv, t=t: T.matmul(pv[:, 0:8], hT[:, k, t * 128:(t + 1) * 128], w[:, k, 0:8], start=(k == 0), stop=(k == 7)))
                              o = of[nxt("of", 2)]
                              op("act", [pv], [o], lambda pv=pv, o=o: A.copy(o[:, 0:8], pv[:, 0:8]))
                              st(dt_d[t * 128:(t + 1) * 128, :], o, o[:, 0:8], q=SQ)
                          fw.barrier()
                      chk("S2")

                  def dense_attn(name, nkh, dq, kT_loads, v_src, head_info, y_row0, scale, padq=False):
                      with contextlib.ExitStack() as s3:
                          kT = fw.sb(s3, name + "kT", [128, nkh, TT], BF16)
                          Vt = fw.sb(s3, name + "V", [128, NT, nkh, 128], BF16)
                          qt = [fw.sb(s3, f"{name}q{i}", [128, 512], BF16) for i in range(6)]
                          gt = [fw.sb(s3, f"{name}g{i}", [128, 512], BF16) for i in range(3)]
                          pt = [fw.sb(s3, f"{name}p{i}", [128, 512], BF16) for i in range(4)]
                          if padq:
                              for i in range(6):
                                  op("pool", [], [qt[i]], lambda i=i: G.memset(qt[i][:], 0.0))
                              if dq == 96:
                                  op("pool", [], [kT], lambda: G.memset(kT[96:128, :, :], 0.0))
                          rd = [fw.sb(s3, f"{name}rd{i}", [128, 512], F32) for i in range(2)]
                          yf = [fw.sb(s3, f"{name}yf{i}", [128, 512], F32) for i in range(2)]
                          yb = [fw.sb(s3, f"{name}yb{i}", [128, 512], BF16) for i in range(3)]
                          QS3 = ("sp", "act", "pool")
                          for li, (r0, r1, i, src) in enumerate(kT_loads):
                              ld(kT, kT[r0:r1, i, :], src, q=QS3[li % 3])
                          op("pool", [], [Vt], lambda: G.memset(Vt[:], 1.0))
                          for i in range(nkh):
                              for b0 in range(0, NT, 9):
                                  b1 = min(NT, b0 + 9)
                                  ld(Vt, Vt[:, b0:b1, i, 0:64], v_src[b0 * 128:b1 * 128, i * 64:(i + 1) * 64].rearrange("(b p) f -> p b f", p=128), q=QS3[(i + b0 // 9) % 3])
                          items = [(gi, t0, wd, h) for gi, (t0, wd) in enumerate(GROUPS) for h in range(4)]

                          def issue_loads(idx):
                              gi, t0, wd, h = items[idx]
                              kh, pr0, vi, qap, gap = head_info(h, t0, wd)
                              qi = (idx % 3) + (3 if pr0 else 0)
                              ld(qt[qi], qt[qi][pr0:pr0 + dq, 0:wd], qap)
                              ld(gt[idx % 3], gt[idx % 3][0:64, 0:wd], gap)

                          LA = 2
                          steps = []
                          for c, (gi, t0, wd, h) in enumerate(items):
                              nkb = 2 if gi == 0 else NT
                              for kb in range(nkb):
                                  steps.append((c, kb, nkb))

                          def item_ctx(c):
                              gi, t0, wd, h = items[c]
                              kh, pr0, vi, qap, gap = head_info(h, t0, wd)
                              Q = qt[(c % 3) + (3 if pr0 else 0)]
                              ka, kb_ = (0, 128) if padq else (pr0, pr0 + dq)
                              return t0, wd, h, kh, vi, Q, gt[c % 3], P[4 + (c % 2)], ka, kb_

                          issue_loads(0)
                          for n in range(len(steps) + LA):
                              if n < len(steps):
                                  c, kb, nkb = steps[n]
                                  t0, wd, h, kh, vi, Q, Gt, O, ka, kb_ = item_ctx(c)
                                  if kb == 0 and c + 1 < len(items):
                                      issue_loads(c + 1)
                                  S = P[n % 4]
                                  Pt = pt[n % 4]
                                  op("pe", [kT, Q], [S], lambda: T.matmul(S[:, 0:wd], kT[ka:kb_, kh, kb * 128:(kb + 1) * 128], Q[ka:kb_, 0:wd], start=True, stop=True))
                                  op("act", [S], [Pt], lambda: A.activation(Pt[:, 0:wd], S[:, 0:wd], AF.Exp, scale=scale))
                              m = n - LA
                              if m >= 0:
                                  c, k2, nkb = steps[m]
                                  t0, wd, h, kh, vi, Q, Gt, O, ka, kb_ = item_ctx(c)
                                  Pt2 = pt[m % 4]
                                  op("pe", [Vt, Pt2], [O], lambda: T.matmul(O[:, 0:wd], Vt[:, k2, vi, :], Pt2[:, 0:wd], start=(k2 == 0), stop=(k2 == nkb - 1)))
                                  if k2 == nkb - 1:
                                      R, Yf, Yb = rd[c % 2], yf[c % 2], yb[c % 3]
                                      op("dve", [O], [R], lambda: V.reciprocal(R[0:64, 0:wd], O[64:128, 0:wd]))
                                      op("dve", [O, R], [Yf], lambda: V.tensor_tensor(Yf[0:64, 0:wd], O[0:64, 0:wd], R[0:64, 0:wd], ALU.mult))
                                      op("pool", [Yf, Gt], [Yb], lambda: G.tensor_tensor(Yb[0:64, 0:wd], Yf[0:64, 0:wd], Gt[0:64, 0:wd], ALU.mult))
                                      st(ycat_rows(y_row0 + h * 64, 64, t0, wd), Yb, Yb[0:64, 0:wd].rearrange("p (t c) -> p t c", c=128))
                          fw.barrier()
                      chk("S3" + name)

                  dense_attn("A", 4, 96,
                             [(0, 64, h, kA_d[h, :, :]) for h in range(4)] + [(64, 96, h, kr_d[:, :]) for h in range(4)],
                             VA_d,
                             lambda h, t0, wd: (h, 0, h, qA_d[h, :, t0:t0 + wd], gA_d[h * 64:(h + 1) * 64, t0:t0 + wd]),
                             0, 96 ** -0.5, padq=True)
                  dense_attn("B", 2, 64,
                             [(0, 128, 0, kB_d[:, :])],
                             VB_d,
                             lambda h, t0, wd: (0, (h // 2) * 64, h // 2, qB_d[h % 2, (h // 2) * 64:(h // 2 + 1) * 64, t0:t0 + wd], gB_d[h * 64:(h + 1) * 64, t0:t0 + wd]),
                             256, 0.125, padq=True)

                  with contextlib.ExitStack() as s3:
                      kT = fw.sb(s3, "CkT", [128, TT], BF16)
                      Vt = fw.sb(s3, "CV", [128, NT, 2, 128], BF16)
                      qt = [fw.sb(s3, f"Cq{i}", [128, 2, 128], BF16) for i in range(8)]
                      gt = [fw.sb(s3, f"Cg{i}", [128, 2, 128], BF16) for i in range(8)]
                      pt = [fw.sb(s3, f"Cp{i}", [128, 640], BF16) for i in range(8)]
                      rd = [fw.sb(s3, f"Crd{i}", [128, 128], F32) for i in range(8)]
                      yf = [fw.sb(s3, f"Cyf{i}", [128, 128], F32) for i in range(8)]
                      yb = [fw.sb(s3, f"Cyb{i}", [128, 128], BF16) for i in range(8)]
                      skb = fw.sb(s3, "skb", [1, 4], F32)
                      skr = fw.sb(s3, "skr", [1, 4, 128], BF16)
                      ld(kT, kT[:, :], kC_d[:, :])
                      op("pool", [], [Vt], lambda: G.memset(Vt[:], 1.0))
                      for g in range(2):
                          for b0 in range(0, NT, 9):
                              b1 = min(NT, b0 + 9)
                              ld(Vt, Vt[:, b0:b1, g, 0:64], VC_d[b0 * 128:b1 * 128, g * 64:(g + 1) * 64].rearrange("(b p) f -> p b f", p=128), q=("sp", "act", "pool")[(g + b0 // 9) % 3])
                      ld(skb, skb[:], sink_in[l, :].partition_broadcast(1))
                      op("act", [skb], [skb], lambda: A.activation(skb[:], skb[:], AF.Exp))
                      op("pool", [], [skr], lambda: G.memset(skr[:], 0.0))
                      op("dve", [skb], [skr], lambda: V.tensor_copy(skr[:, :, 64:128], skb[:, :].unsqueeze(2).to_broadcast([1, 4, 64])))
                      def c_block(qb, slot):
                          Q = qt[qb % 8]
                          Gt = gt[qb % 8]
                          ld(Q, Q[:, :, :], qC_d[:, :, qb * 128:(qb + 1) * 128].rearrange("j p t -> p j t"))
                          ld(Gt, Gt[:, :, :], gC_d[:, qb * 128:(qb + 1) * 128].rearrange("(j p) t -> p j t", p=128))
                          kbs = [(0, None), (1, None)]
                          if qb >= 2:
                              n = qb - 2
                              if n > 0:
                                  kbs.append((qb - 1, 2))
                              kbs.append((qb, None))
                              if n < 31:
                                  kbs.append((qb + 1, 1))
                          nk = len(kbs)
                          yield
                          for h in range(4):
                              g, j = h // 2, h % 2
                              Yb = yb[slot * 2 + g]
                              Sa = P[slot * 2]
                              SbO = P[slot * 2 + 1]
                              Pt = pt[slot * 2 + (h % 2)]
                              for i, (kb, _) in enumerate(kbs):
                                  Sr = Sa if i < 4 else SbO
                                  dst = Sa[:, i * 128:(i + 1) * 128] if i < 4 else SbO[:, 0:128]
                                  op("pe", [kT, Q], [Sr], lambda dst=dst, kb=kb, g=g, j=j: T.matmul(dst, kT[g * 64:(g + 1) * 64, kb * 128:(kb + 1) * 128], Q[g * 64:(g + 1) * 64, j, :], start=True, stop=True))
                              yield
                              na = min(nk, 4)
                              op("act", [Sa], [Pt], lambda: A.activation(Pt[:, 0:na * 128], Sa[:, 0:na * 128], AF.Exp, scale=0.125))
                              if nk > 4:
                                  op("act", [SbO], [Pt], lambda: A.activation(Pt[:, 512:640], SbO[:, 0:128], AF.Exp, scale=0.125))
                              yield
                              for i, (kb, m) in enumerate(kbs):
                                  if m is not None:
                                      op("pool", [Pt, cbf], [Pt], lambda i=i, m=m: G.tensor_tensor(Pt[:, i * 128:(i + 1) * 128], Pt[:, i * 128:(i + 1) * 128], cbf[:, m, :], ALU.mult))
                              yield
                              for i, (kb, _) in enumerate(kbs):
                                  op("pe", [Vt, Pt], [SbO], lambda i=i, kb=kb: T.matmul(SbO[:, 128:256], Vt[:, kb, g, :], Pt[:, i * 128:(i + 1) * 128], start=(i == 0), stop=False))
                              op("pe", [skr, cbf], [SbO], lambda: T.matmul(SbO[:, 128:256], skr[0:1, h, :], cbf[0:1, 3, :], start=False, stop=True))
                              yield
                              R, Yf = rd[slot * 2 + (h % 2)], yf[slot * 2 + (h % 2)]
                              op("dve", [SbO], [R], lambda: V.reciprocal(R[j * 64:(j + 1) * 64, :], SbO[64:128, 128:256]))
                              yield
                              op("dve", [SbO, R], [Yf], lambda: V.tensor_tensor(Yf[j * 64:(j + 1) * 64, :], SbO[0:64, 128:256], R[j * 64:(j + 1) * 64, :], ALU.mult))
                              yield
                              op("pool", [Yf, Gt], [Yb], lambda: G.tensor_tensor(Yb[j * 64:(j + 1) * 64, :], Yf[j * 64:(j + 1) * 64, :], Gt[j * 64:(j + 1) * 64, g, :], ALU.mult))
                              if j == 1:
                                  st(ycat_d[qb, :, 4 + g, :], Yb, Yb[:, :], q="act")
                              yield

                      run_window(((lambda slot, qb=qb: c_block(qb, slot)) for qb in range(NT)), 4)
                      fw.barrier()
                  chk("S3C")

                  with contextlib.ExitStack() as s4:
                      uT = fw.sb(s4, "uT", [128, 6, TT], BF16)
                      yacc = [fw.sb(s4, f"yacc{d}", [128, 2, TT], F32) for d in range(2)]
                      with contextlib.ExitStack() as s4a:
                          xin = [fw.sb(s4a, f"xin{i}", [128, 516], F32) for i in range(3)]
                          cva = [fw.sb(s4a, f"cva{i}", [128, 512], F32) for i in range(3)]
                          cve = [fw.sb(s4a, f"cve{i}", [128, 512], F32) for i in range(2)]
                          it = 0
                          for gi, (t0, wd) in enumerate(GROUPS):
                              s_lo, s_hi = (0, NCTX) if gi == 0 else (NCTX, TT)
                              lo, hi = t0 - 2, t0 + wd + 2
                              clo, chi = max(lo, s_lo), min(hi, s_hi)
                              for tl in range(6):
                                  X = xin[it % 3]
                                  acc = cva[it % 3]
                                  e_ = cve[it % 2]
                                  it += 1
                                  if clo > lo or chi < hi:
                                      op("pool", [], [X], lambda X=X: G.memset(X[:, 0:wd + 4], 0.0))
                                  ld(X, X[:, clo - lo:chi - lo], xbc_d[tl * 128:(tl + 1) * 128, clo:chi])
                                  cw = 7 + tl * 5
                                  op("dve", [X, colv], [acc], lambda X=X, acc=acc, cw=cw, tl=tl: V.tensor_scalar(acc[:, 0:wd], X[:, 0:wd], colv[:, cw:cw + 1], colv[:, 37 + tl:38 + tl], ALU.mult, ALU.add))
                                  for k in range(1, 5):
                                      eng = "dve"
                                      E_ = V
                                      op(eng, [X, colv, acc], [acc], lambda X=X, acc=acc, cw=cw, k=k, E_=E_: E_.scalar_tensor_tensor(acc[:, 0:wd], X[:, k:k + wd], colv[:, cw + k:cw + k + 1], acc[:, 0:wd], ALU.mult, ALU.add))
                                  op("act", [acc], [uT], lambda acc=acc, tl=tl: A.activation(uT[:, tl, t0:t0 + wd], acc[:, 0:wd], AF.Silu))
                          fw.barrier()
                      with contextlib.ExitStack() as s4b:
                          arow = fw.sb(s4b, "arow", [128, 8], F32)
                          brow = fw.sb(s4b, "brow", [128, 8], F32)
                          NSET = 2
                          Sst = [fw.sb(s4b, f"Sst{d}", [128, 4, 64], F32) for d in range(2)]
                          Spad = [fw.sb(s4b, f"Spad{d}", [128, 4, 128], BF16) for d in range(2)]
                          def mk(nm, shape, dt):
                              return [[fw.sb(s4b, f"{nm}{d}_{i}", shape, dt) for i in range(NSET)] for d in range(2)]
                          xdp = mk("xdp", [128, 4, 128], BF16)
                          xdw = mk("xdw", [128, 4, 64], BF16)
                          dtr = mk("dtr", [128, 4], F32)
                          sm = mk("sm", [128, 8, 4], F32)
                          cs = mk("cs", [128, 8], F32)
                          dAb = mk("dAb", [128, 4, 128], F32)
                          seg = mk("seg", [128, 4, 128], F32)
                          EA = mk("EA", [128, 4, 128], F32)
                          CBm = mk("CBm", [128, 2, 128], F32)
                          scr = mk("scr", [128, 4, 128], BF16)
                          cdc = mk("cdc", [128, 4, 128], BF16)
                          xtk = mk("xtk", [128, 512], BF16)
                          ld(arow, arow[:], alog_in[l, :].partition_broadcast(128))
                          ld(brow, brow[:], dtb_in[l, :].partition_broadcast(128))
                          op("act", [arow], [arow], lambda: A.activation(arow[:], arow[:], AF.Exp))
                          op("dve", [arow], [arow], lambda: V.tensor_scalar_mul(arow[:], arow[:], -1.0))
                          for d in range(2):
                              for i in range(NSET):
                                  op("pool", [], [xdp[d][i]], lambda d=d, i=i: G.memset(xdp[d][i][:], 0.0))
                              op("pool", [], [Sst[d]], lambda d=d: G.memset(Sst[d][:], 0.0))
                              op("pool", [], [Spad[d]], lambda d=d: G.memset(Spad[d][:], 0.0))
                          orders = [list(range(NT)), [1, 0] + list(range(NT - 1, 1, -1))]

                          def bview(ap2, n):
                              return ap2.unsqueeze(2)

                          def ssd_P(d, ci, ch):
                              tri = 1 if d == 0 else 2
                              i = ci % NSET
                              c0 = ch * 128
                              pA, pB, pC, pD = P[4 * d], P[4 * d + 1], P[4 * d + 2], P[4 * d + 3]
                              DT, SM, CS = dtr[d][i], sm[d][i], cs[d][i]
                              ld(DT, DT[:], dt_d[c0:c0 + 128, d * 4:(d + 1) * 4])
                              ptrb = pB.t.bitcast(BF16)
                              for q_ in range(4):
                                  op("pe", [uT, cbf], [pB], lambda q_=q_: T.transpose(ptrb[:, q_ * 128:(q_ + 1) * 128], uT[:, q_, c0:c0 + 128], cbf[:, 0, :]))
                              for g in range(2):
                                  op("pe", [uT], [pC], lambda g=g: T.matmul(pC[:, g * 128:(g + 1) * 128], uT[:, 2 + g, c0:c0 + 128], uT[:, 4 + g, c0:c0 + 128], start=True, stop=True))
                              yield
                              op("dve", [DT, brow], [SM], lambda: V.tensor_tensor(SM[:, 0, :], DT[:], brow[:, d * 4:(d + 1) * 4], ALU.add))
                              XT = xtk[d][i]
                              op("act", [pB], [XT], lambda: A.copy(XT[:, 0:512], ptrb[:, 0:512]))
                              yield
                              op("act", [SM], [SM], lambda: A.activation(SM[:, 1, :], SM[:, 0, :], AF.Abs))
                              CB = CBm[d][i]
                              op("dve", [pC, consts], [CB], lambda: V.tensor_tensor(CB[:], pC[:, 0:256].rearrange("p (g t) -> p g t", g=2), consts[:, tri, :].unsqueeze(1).to_broadcast([128, 2, 128]), ALU.mult))
                              yield
                              op("act", [SM], [SM], lambda: A.activation(SM[:, 1, :], SM[:, 1, :], AF.Exp, scale=-1.0))
                              yield
                              op("dve", [SM], [SM], lambda: V.tensor_scalar_add(SM[:, 1, :], SM[:, 1, :], 1.0))
                              yield
                              op("act", [SM], [SM], lambda: A.activation(SM[:, 2, :], SM[:, 1, :], AF.Ln))
                              yield
                              op("dve", [SM], [SM], lambda: V.scalar_tensor_tensor(SM[:, 3, :], SM[:, 0, :], 0.0, SM[:, 2, :], ALU.max, ALU.add))
                              yield
                              op("dve", [SM, arow], [SM], lambda: V.tensor_tensor(SM[:, 4, :], SM[:, 3, :], arow[:, d * 4:(d + 1) * 4], ALU.mult))
                              yield
                              op("pe", [consts, SM], [pB], lambda: T.matmul(pB[:, 256:260], consts[:, tri, :], SM[:, 4, :], start=True, stop=True))
                              op("pe", [consts, SM], [pB], lambda: T.matmul(pB[:, 260:264], consts[:, 3, :], SM[:, 4, :], start=True, stop=True))
                              DB = dAb[d][i]
                              op("dve", [consts, SM], [DB], lambda: V.tensor_tensor(DB[:], consts[:, 3, :].unsqueeze(1).to_broadcast([128, 4, 128]), SM[:, 4, :].unsqueeze(2).to_broadcast([128, 4, 128]), ALU.mult))
                              yield
                              op("act", [pB], [CS], lambda: A.copy(CS[:, 0:8], pB[:, 256:264]))
                              for h in range(4):
                                  op("pe", [DB, consts], [pA], lambda h=h: T.matmul(pA[:, h * 128:(h + 1) * 128], DB[:, h, :], consts[:, tri, :], start=True, stop=True))
                              yield
                              SG, EAi = seg[d][i], EA[d][i]
                              for h in range(4):
                                  op("dve", [pA, CS], [SG], lambda h=h: V.tensor_scalar(SG[:, h, :], pA[:, h * 128:(h + 1) * 128], CS[:, h:h + 1], 0.0, ALU.subtract, ALU.min))
                              yield
                              op("act", [pA], [EAi], lambda: A.activation(EAi[:].rearrange("p h t -> p (h t)"), pA[:, 0:512], AF.Exp))
                              op("dve", [CS], [SM], lambda: V.tensor_tensor(SM[:, 5, :], CS[:, 4:8], CS[:, 0:4], ALU.subtract))
                              yield
                              op("act", [SG], [SG], lambda: A.activation(SG[:], SG[:], AF.Exp))
                              yield
                              op("act", [SM], [SM], lambda: A.activation(SM[:, 5, :], SM[:, 5, :], AF.Exp))
                              op("act", [CS], [SM], lambda: A.activation(SM[:, 6, :], CS[:, 4:8], AF.Exp))
                              SC, CD = scr[d][i], cdc[d][i]
                              for g in range(2):
                                  op("pool", [EAi, uT], [CD], lambda g=g: G.tensor_tensor(CD[:, 2 * g:2 * g + 2, :], EAi[:, 2 * g:2 * g + 2, :], uT[:, 4 + g, c0:c0 + 128].unsqueeze(1).to_broadcast([128, 2, 128]), ALU.mult))
                              yield
                              for g in range(2):
                                  op("dve", [SG, CB], [SC], lambda g=g: V.tensor_tensor(SC[:, 2 * g:2 * g + 2, :], SG[:, 2 * g:2 * g + 2, :], CB[:, g, :].unsqueeze(1).to_broadcast([128, 2, 128]), ALU.mult))
                              op("dve", [SM], [SM], lambda: V.tensor_tensor(SM[:, 7, :], SM[:, 3, :], SM[:, 5, :], ALU.mult))
                              yield
                              XP, XW = xdp[d][i], xdw[d][i]
                              for j in range(2):
                                  op("dve", [XT, SM], [XP], lambda j=j: V.tensor_tensor(XP[:, j::2, j * 64:(j + 1) * 64], XT[:, 0:256].rearrange("p (h f) -> p h f", h=4)[:, j::2, :], SM[:, 3, j::2].unsqueeze(2).to_broadcast([128, 2, 64]), ALU.mult))
                              op("dve", [XT, SM], [XW], lambda: V.tensor_tensor(XW[:], XT[:, 0:256].rearrange("p (h f) -> p h f", h=4), SM[:, 7, :].unsqueeze(2).to_broadcast([128, 4, 64]), ALU.mult))
                              yield
                              for h in range(4):
                                  g = h // 2
                                  op("pe", [XT, XW], [pD], lambda h=h, g=g: T.matmul(pD[:, 256 + h * 64:256 + (h + 1) * 64], XT[:, 256 + g * 128:256 + (g + 1) * 128], XW[:, h, :], start=True, stop=True))
                              yield

                          def ssd_Q(d, ci, ch):
                              i = ci % NSET
                              c0 = ch * 128
                              pD = P[4 * d + 3]
                              SM = sm[d][i]
                              XP, SC, CD = xdp[d][i], scr[d][i], cdc[d][i]
                              for tl in range(2):
                                  for j in range(2):
                                      h = 2 * tl + j
                                      op("pe", [XP, SC], [pD], lambda h=h, tl=tl, j=j: T.matmul(pD[:, tl * 128:(tl + 1) * 128], XP[:, h, :], SC[:, h, :], start=(j == 0), stop=False))
                                      op("pe", [Spad[d], CD], [pD], lambda h=h, tl=tl, j=j: T.matmul(pD[:, tl * 128:(tl + 1) * 128], Spad[d][:, h, :], CD[:, h, :], start=False, stop=(j == 1)))
                              yield
                              op("dve", [Sst[d], SM], [Sst[d]], lambda: V.tensor_tensor(Sst[d][:], Sst[d][:], SM[:, 6, :].unsqueeze(2).to_broadcast([128, 4, 64]), ALU.mult))
                              yield
                              op("dve", [Sst[d], pD], [Sst[d]], lambda: V.tensor_tensor(Sst[d][:], Sst[d][:], pD[:, 256:512].rearrange("p (h f) -> p h f", h=4), ALU.add))
                              yield
                              if d == 0:
                                  for tl in range(2):
                                      op("dve", [uT, colv, pD], [yacc[0]], lambda tl=tl: V.scalar_tensor_tensor(yacc[0][:, tl, c0:c0 + 128], uT[:, tl, c0:c0 + 128], colv[:, 43 + tl:44 + tl], pD[:, tl * 128:(tl + 1) * 128], ALU.mult, ALU.add))
                              else:
                                  op("act", [pD], [yacc[1]], lambda: A.copy(yacc[1][:, :, c0:c0 + 128], pD[:, 0:256].rearrange("p (a t) -> p a t", a=2)))
                              for j in range(2):
                                  op("pool", [Sst[d]], [Spad[d]], lambda j=j: G.tensor_copy(Spad[d][:, j::2, j * 64:(j + 1) * 64], Sst[d][:, j::2, :]))
                              yield

                          run_chains([ssd_P(0, 0, orders[0][0]), ssd_P(1, 0, orders[1][0])])
                          for ci in range(NT):
                              gl = [ssd_Q(0, ci, orders[0][ci]), ssd_Q(1, ci, orders[1][ci])]
                              if ci + 1 < NT:
                                  gl += [ssd_P(0, ci + 1, orders[0][ci + 1]), ssd_P(1, ci + 1, orders[1][ci + 1])]
                              run_chains(gl)
                          fw.barrier()
                      if dbg and "dbgu_d" in dbg:
                          for tl in range(6):
                              st(dbgu_d[tl * 128:(tl + 1) * 128, :], uT, uT[:, tl, :])
                          for tl in range(2):
                              st(dbgy_d[tl * 128:(tl + 1) * 128, :], yacc[0], yacc[0][:, tl, :])
                      with contextlib.ExitStack() as s4c:
                          zt = [fw.sb(s4c, f"zt{i}", [128, 512], BF16) for i in range(2)]
                          yz = [fw.sb(s4c, f"yz{i}", [128, 512], F32) for i in range(2)]
                          sq2 = fw.sb(s4c, "sq2", [128, 2, 512], F32)
                          rs2 = fw.sb(s4c, "rs2", [128, 512], F32)
                          yo = [fw.sb(s4c, f"yo{i}", [128, 512], BF16) for i in range(2)]
                          for (t0, wd) in GROUPS:
                              for tl in range(2):
                                  ld(zt[tl], zt[tl][:, 0:wd], zs_d[tl * 128:(tl + 1) * 128, t0:t0 + wd])
                                  op("pool", [yacc[0], yacc[1]], [yz[tl]], lambda tl=tl: G.tensor_tensor(yz[tl][:, 0:wd], yacc[0][:, tl, t0:t0 + wd], yacc[1][:, tl, t0:t0 + wd], ALU.add))
                                  op("dve", [yz[tl], zt[tl]], [yz[tl]], lambda tl=tl: V.tensor_tensor(yz[tl][:, 0:wd], yz[tl][:, 0:wd], zt[tl][:, 0:wd], ALU.mult))
                                  op("act", [yz[tl]], [sq2], lambda tl=tl: A.activation(sq2[:, tl, 0:wd], yz[tl][:, 0:wd], AF.Square))
                              pss = P[6]
                              for tl in range(2):
                                  op("pe", [sq2, consts], [pss], lambda tl=tl: T.matmul(pss[:, 0:wd], consts[:, 3, :], sq2[:, tl, 0:wd], start=(tl == 0), stop=(tl == 1)))
                              rstd_from(pss, pss[:, 0:wd], rs2, rs2[:, 0:wd], 256)
                              for tl in range(2):
                                  op("dve", [yz[tl], colv, rs2], [yo[tl]], lambda tl=tl: V.scalar_tensor_tensor(yo[tl][:, 0:wd], yz[tl][:, 0:wd], colv[:, 45 + tl:46 + tl], rs2[:, 0:wd], ALU.mult, ALU.mult))
                                  st(ycat_rows(768 + tl * 128, 128, t0, wd), yo[tl], yo[tl][:, 0:wd].rearrange("p (t c) -> p t c", c=128))
                      fw.barrier()
                  chk("S4")

                  with contextlib.ExitStack() as s5:
                      wos = fw.sb(s5, "wos", [128, 8, 512], F32)
                      wo = fw.sb(s5, "wo", [128, 8, D], BF16)
                      gb = [fw.sb(s5, f"gb{j}", [128, D], F32) for j in range(2)]
                      fnw = fw.sb(s5, "fnw", [128, D], F32)
                      yc = [fw.sb(s5, f"yc{i}", [128, 8, 128], BF16) for i in range(3)]
                      xs = [fw.sb(s5, f"xs{i}", [128, D], F32) for i in range(3)]
                      tm = [fw.sb(s5, f"tm{i}", [128, D], F32) for i in range(3)]
                      xn = [fw.sb(s5, f"xn{i}", [128, D], F32) for i in range(3)]
                      junk5 = fw.sb(s5, "junk5", [128, D], BF16)
                      s5s = [fw.sb(s5, f"s5s{i}", [128, 1], F32) for i in range(3)]
                      for half in range(2):
                          ld(wos, wos[:], w_out[l, :, half * 512:(half + 1) * 512].rearrange("(k p) c -> p k c", p=128))
                          op("pool", [wos], [wo], lambda half=half: G.tensor_copy(wo[:, :, half * 512:(half + 1) * 512], wos[:]))
                      for j in range(2):
                          ld(gb[j], gb[j][:], mod_d[l, j, 2 * D:3 * D].partition_broadcast(128))
                      if last:
                          ld(fnw, fnw[:], fnw_in.partition_broadcast(128))
                      tiles5 = [t for t in range(NT) if not (last and t < 2)]

                      def loads5(t):
                          YC, X = yc[t % 3], xs[t % 3]
                          ld(YC, YC[:], ycat_d[t, :, :, :])
                          if l == 0:
                              src = ctx_in[t * 128:(t + 1) * 128, :] if t < 2 else x_in[(t - 2) * 128:(t - 1) * 128, :]
                          else:
                              src = xres_d[t * 128:(t + 1) * 128, :]
                          ld(X, X[:], src, q="pool")

                      def s5_tile(t):
                          j = 1 if t < 2 else 0
                          YC, X, TM, XN = yc[t % 3], xs[t % 3], tm[t % 3], xn[t % 3]
                          loads5(t)
                          yield
                          pos = [P[(2 * (t % 3)) + half] for half in range(2)]
                          for half in range(2):
                              po = pos[half]
                              for k in range(8):
                                  op("pe", [YC, wo], [po], lambda k=k, half=half, po=po: T.matmul(po[:, 0:512], YC[:, k, :], wo[:, k, half * 512:(half + 1) * 512], start=(k == 0), stop=(k == 7)))
                          yield
                          for half in range(2):
                              po = pos[half]
                              op("dve", [po, gb[j]], [TM], lambda half=half, po=po: V.tensor_tensor(TM[:, half * 512:(half + 1) * 512], po[:, 0:512], gb[j][:, half * 512:(half + 1) * 512], ALU.mult))
                          yield
                          op("dve", [TM, X], [XN], lambda: V.tensor_tensor(XN[:], TM[:], X[:], ALU.add))
                          yield
                          if not last:
                              st(xres_d[t * 128:(t + 1) * 128, :], XN, XN[:], q="act")
                          else:
                              S_ = s5s[t % 3]
                              op("pool", [], [S_], lambda: G.memset(S_[:], 0.0))
                              op("act", [XN], [junk5, S_], lambda: A.activation(junk5[:], XN[:], AF.Square, accum_out=S_[:, 0:1]))
                              yield
                              rstd_from(S_, S_[:, 0:1], S_, S_[:, 0:1], D)
                              yield
                              op("dve", [XN, S_, fnw], [TM], lambda: V.scalar_tensor_tensor(TM[:], XN[:], S_[:, 0:1], fnw[:], ALU.mult, ALU.mult))
                              st(out_d[(t - 2) * 128:(t - 1) * 128, :], TM, TM[:], q="act")
                          yield

                      run_window((s5_tile(t) for t in tiles5), 3)
                      fw.barrier()
        except _Stop:
            pass
        fw.barrier(force=True)
        print("instructions:", fw.ninstr, "sems:", fw.nsem)
    return nc


def _tables():
    f32 = np.float32
    n = np.arange(NLAT)
    row = (n // 64).astype(f32)
    col = (n % 64).astype(f32)

    def ang(rot_dim):
        nf = rot_dim // 4
        inv = (f32(10000.0) ** (-(np.arange(nf, dtype=f32) / f32(nf)))).astype(f32)
        return np.concatenate([row[:, None] * inv, col[:, None] * inv], axis=-1).astype(f32)

    a64 = ang(64)
    a32 = ang(32)
    cos64 = np.ones((128, TT), f32)
    sin64 = np.zeros((128, TT), f32)
    for r in range(128):
        d = r % 64
        cos64[r, NCTX:] = np.cos(a64[:, d % 32])
        sin64[r, NCTX:] = (-1.0 if d < 32 else 1.0) * np.sin(a64[:, d % 32])
    cos96 = np.ones((128, TT), f32)
    sin96 = np.zeros((128, TT), f32)
    for r in range(64, 96):
        d = r - 64
        cos96[r, NCTX:] = np.cos(a32[:, d % 16])
        sin96[r, NCTX:] = (-1.0 if d < 16 else 1.0) * np.sin(a32[:, d % 16])
    cos96[96:] = 0.0
    k = np.arange(128)
    consts = np.zeros((128, 5, 128), f32)
    consts[:, 0] = np.eye(128, dtype=f32)
    consts[:, 1] = (k[:, None] <= k[None, :]).astype(f32)
    consts[:, 2] = (k[:, None] >= k[None, :]).astype(f32)
    consts[:, 3] = 1.0
    consts[:, 4] = ((k[:, None] // 64) == (k[None, :] // 64)).astype(f32)
    return cos64, sin64, cos96, sin96, consts


def _col_index():
    p64 = [(d + 32) % 64 for d in range(64)]
    p32 = [(d + 16) % 32 for d in range(32)]
    blocks = []
    blocks.append(list(range(0, 128)))
    blocks.append(list(range(128, 256)))
    blocks.append(list(range(256, 384)))
    blocks.append([-1] * 64 + [384 + d for d in range(32)] + [-1] * 32)
    blocks.append([-1] * 64 + [384 + p32[d] for d in range(32)] + [-1] * 32)
    blocks.append(list(range(416, 544)))
    blocks.append(list(range(544, 672)))
    for base in (672, 1440):
        q0, k0, v0, g0 = base, base + 256, base + 384, base + 512
        for j in range(2):
            blocks.append([q0 + j * 64 + d for d in range(64)] + [q0 + (2 + j) * 64 + d for d in range(64)])
        for j in range(2):
            blocks.append([q0 + j * 64 + p64[d] for d in range(64)] + [q0 + (2 + j) * 64 + p64[d] for d in range(64)])
        blocks.append(list(range(k0, k0 + 128)))
        blocks.append([k0 + g * 64 + p64[d] for g in range(2) for d in range(64)])
        blocks.append(list(range(v0, v0 + 128)))
        blocks.append(list(range(g0, g0 + 128)))
        blocks.append(list(range(g0 + 128, g0 + 256)))
    blocks.append(list(range(2208, 2336)))
    blocks.append(list(range(2336, 2464)))
    for i in range(6):
        blocks.append(list(range(2464 + i * 128, 2464 + (i + 1) * 128)))
    blocks.append(list(range(3232, 3240)) + [-1] * 120)
    assert len(blocks) == NBLK
    return np.array([c for b in blocks for c in b], dtype=np.int64)


_CACHE = {}


def _prep_shared(inp):
    f32 = np.float32
    idx = _col_index()
    w_in = np.asarray(inp["w_in"], f32)
    w_in_z = np.concatenate([w_in, np.zeros((L, D, 1), f32)], axis=-1)
    w_in_p = np.ascontiguousarray(w_in_z[:, :, np.where(idx < 0, 3240, idx)])
    wq = np.asarray(inp["mla_w_q_up"], f32)
    pA = np.array([h * 96 + (d if d < 64 else 64 + (d - 64 + 16) % 32) for h in range(4) for d in range(96)])
    wq_up_p = np.ascontiguousarray(np.stack([wq, wq[:, :, pA]], axis=2))
    wkv = np.asarray(inp["mla_w_kv_up"], f32).reshape(L, 128, 4, 128)
    wkv_k = np.ascontiguousarray(wkv[:, :, :, 0:64].reshape(L, 128, 256))
    wkv_v = np.ascontiguousarray(wkv[:, :, :, 64:128].reshape(L, 128, 256))
    colv = np.zeros((L, 128, NV), f32)
    p = np.arange(128)
    p64 = (p % 64 + 32) % 64
    qn, kn = np.asarray(inp["gqa_q_norm"], f32), np.asarray(inp["gqa_k_norm"], f32)
    colv[:, :, 0] = qn[:, p % 64]
    colv[:, :, 1] = qn[:, p64]
    colv[:, :, 2] = kn[:, p % 64]
    colv[:, :, 3] = kn[:, p64]
    mq = np.asarray(inp["mla_q_norm"], f32)
    colv[:, :, 4] = mq[:, 0:128]
    colv[:, :, 5] = mq[:, 128:256]
    colv[:, :, 6] = np.asarray(inp["mla_kv_norm"], f32)
    cw = np.asarray(inp["ssd_conv_w"], f32)
    cb = np.asarray(inp["ssd_conv_b"], f32)
    for t in range(6):
        for k in range(5):
            colv[:, :, 7 + t * 5 + k] = cw[:, k, t * 128:(t + 1) * 128]
        colv[:, :, 37 + t] = cb[:, t * 128:(t + 1) * 128]
    sd = np.asarray(inp["ssd_d"], f32)
    snw = np.asarray(inp["ssd_norm_w"], f32)
    for tl in range(2):
        colv[:, :, 43 + tl] = sd[:, tl * 2 + p // 64]
        colv[:, :, 45 + tl] = snw[:, tl * 128:(tl + 1) * 128]
    cos64, sin64, cos96, sin96, consts = _tables()
    return {
        "w_mod": np.asarray(inp["w_mod"], f32), "b_mod": np.asarray(inp["b_mod"], f32),
        "norm_w": np.asarray(inp["norm_w"], f32), "w_in_p": w_in_p, "wq_up_p": wq_up_p,
        "wkv_k": wkv_k, "wkv_v": wkv_v, "colv": colv,
        "a_log": np.ascontiguousarray(np.asarray(inp["ssd_a_log"], f32).reshape(L, 8)),
        "dt_bias": np.ascontiguousarray(np.asarray(inp["ssd_dt_bias"], f32).reshape(L, 8)),
        "sink": np.asarray(inp["swa_sink"], f32), "w_out": np.asarray(inp["w_out"], f32),
        "final_norm_w": np.asarray(inp["final_norm_w"], f32), "consts": consts,
        "cos64": cos64, "sin64": sin64, "cos96": cos96, "sin96": sin96,
    }


def make_in_maps(inp, ncores=8):
    shared = _prep_shared(inp)
    x = np.asarray(inp["x"], np.float32)
    ctx = np.asarray(inp["ctx"], np.float32)
    c = np.asarray(inp["c"], np.float32)
    c_ctx = np.asarray(inp["c_ctx"], np.float32)
    maps = []
    for i in range(ncores):
        b = i % 4
        ccT = np.ascontiguousarray(np.stack([c[b], c_ctx], axis=-1).reshape(8, 128, 2).transpose(1, 0, 2))
        m = dict(shared)
        m.update({"x": np.ascontiguousarray(x[b]), "ctx": np.ascontiguousarray(ctx[b]), "ccT": ccT})
        maps.append(m)
    return maps


def kernel(**inputs):
    if "nc" not in _CACHE:
        _CACHE["nc"] = build()
    nc = _CACHE["nc"]
    maps = make_in_maps(inputs)
    res = run_bass_kernel_spmd(nc, maps, core_ids=list(range(8)))
    out = np.stack([np.asarray(res.results[b]["out"], np.float32) for b in range(4)], axis=0)
    return out
```

```python
import contextlib
import numpy as np
import concourse.bass as bass
import concourse.mybir as mybir
from concourse.bass_utils import run_bass_kernel_spmd

F32 = mybir.dt.float32
BF16 = mybir.dt.bfloat16
AF = mybir.ActivationFunctionType
ALU = mybir.AluOpType

L = 4
D = 1024
SQ = "act"
NCTX = 256
NLAT = 4096
TT = NCTX + NLAT
NT = TT // 128
EPS = 1e-6
GROUPS = [(0, 256)] + [(256 + 512 * i, 512) for i in range(8)]
NBLK = 34
NCOLS = NBLK * 128
NV = 56


class Res:
    __slots__ = ("name", "w", "r", "t", "psum")

    def __init__(self, name, t=None, psum=False):
        self.name = name
        self.w = None
        self.r = []
        self.t = t
        self.psum = psum

    def __getitem__(self, k):
        return self.t[k]


class FW:
    EPOCH = 30000

    def __init__(self, nc, es):
        self.nc = nc
        self.es = es
        self.eng = {"pe": nc.tensor, "act": nc.scalar, "dve": nc.vector, "pool": nc.gpsimd, "sp": nc.sync}
        self.esem = {}
        self.ecnt = {}
        self.own = {k: set() for k in self.eng}
        self.waited = {k: {} for k in self.eng}
        self.dsems = {}
        self.free_dsems = []
        self.nsem = 0
        self.stopped = False
        self.uid = 0
        self.ninstr = 0
        for k in ("pe", "act", "dve", "pool"):
            self._new_epoch(k)

    def _sem(self, name):
        self.nsem += 1
        return self.es.enter_context(self.nc.semaphore(f"{name}_{self.nsem}"))

    def _new_epoch(self, k):
        self.esem[k] = self._sem("e" + k)
        self.own[k].add(id(self.esem[k]))
        self.ecnt[k] = 0

    def sb(self, st, name, shape, dt):
        self.uid += 1
        return Res(name, st.enter_context(self.nc.sbuf_tensor(f"sb_{name}_{self.uid}", list(shape), dt)))

    def ps(self, st, name, shape, dt=F32):
        self.uid += 1
        return Res(name, st.enter_context(self.nc.psum_tensor(f"ps_{name}_{self.uid}", list(shape), dt)), psum=True)

    def _wait(self, e, tok):
        if tok is None:
            return
        sem, val = tok
        key = id(sem)
        if e == "pe" and key in self.own["pe"]:
            return
        prev = self.waited[e].get(key)
        if prev is not None and prev >= val:
            return
        self.eng[e].wait_ge(sem, val)
        self.waited[e][key] = val

    def _deps(self, e, reads, writes):
        for r in reads:
            self._wait(e, r.w)
            if r.psum:
                for tok in r.r:
                    self._wait(e, tok)
        for w in writes:
            self._wait(e, w.w)
            for tok in w.r:
                self._wait(e, tok)

    def _commit(self, tok, reads, writes):
        for r in reads:
            if r in writes:
                continue
            r.r.append(tok)
            if len(r.r) > 16:
                d = {}
                for s, v in r.r:
                    k = id(s)
                    if k not in d or d[k][1] < v:
                        d[k] = (s, v)
                r.r = list(d.values())
        for w in writes:
            w.w = tok
            w.r = []

    def op(self, e, reads, writes, fn):
        if self.stopped:
            return None
        self._deps(e, reads, writes)
        ins = fn()
        if self.ecnt[e] >= self.EPOCH:
            self._new_epoch(e)
        self.ecnt[e] += 1
        tok = (self.esem[e], self.ecnt[e])
        ins.then_inc(tok[0], 1)
        self._commit(tok, reads, writes)
        self.ninstr += 1
        return ins

    def dma(self, q, out_ap, in_ap, reads, writes, key, **kw):
        if self.stopped:
            return None
        self._deps(q, reads, writes)
        if key not in self.dsems:
            self.dsems[key] = self.free_dsems.pop() if self.free_dsems else [self._sem("d"), 0]
        ent = self.dsems[key]
        ins = self.eng[q].dma_start(out=out_ap, in_=in_ap, **kw)
        ent[1] += 16
        tok = (ent[0], ent[1])
        ins.then_inc(tok[0], 16)
        self._commit(tok, reads, writes)
        self.ninstr += 1
        return tok

    def ld(self, res, sb_ap, dram_ap, q="sp", extra_reads=(), **kw):
        return self.dma(q, sb_ap, dram_ap, list(extra_reads), [res], "L" + res.name, **kw)

    def st(self, dram_ap, res, sb_ap, q="sp", extra_writes=(), **kw):
        return self.dma(q, dram_ap, sb_ap, [res], list(extra_writes), "S" + res.name, **kw)

    def barrier(self, force=False):
        if self.stopped and not force:
            return
        toks = [(self.esem[k], self.ecnt[k]) for k in ("pe", "act", "dve", "pool") if self.ecnt[k] > 0]
        toks += [(s, c) for s, c in self.dsems.values() if c > 0]
        for e in ("pe", "act", "dve", "pool", "sp"):
            for t in toks:
                if e == "pe" and id(t[0]) in self.own["pe"]:
                    continue
                self._wait(e, t)
        self.free_dsems.extend(self.dsems.values())
        self.dsems = {}


class _Stop(Exception):
    pass


def run_window(gen_iter, width):
    it = iter(gen_iter)

    def start(idx):
        f = next(it, None)
        if f is None:
            return None
        return f(idx) if callable(f) else f

    active = [start(i) for i in range(width)]
    while any(g_ is not None for g_ in active):
        for idx in range(width):
            g_ = active[idx]
            if g_ is None:
                continue
            try:
                next(g_)
            except StopIteration:
                active[idx] = start(idx)


def run_chains(gens):
    gens = list(gens)
    while gens:
        for g_ in list(gens):
            try:
                next(g_)
            except StopIteration:
                gens.remove(g_)


def build(nlayers=L, dbg=None, stop=None):
    nc = bass.Bass("TRN2", target_bir_lowering=False)
    V, A, G, T = nc.vector, nc.scalar, nc.gpsimd, nc.tensor

    def din(name, shape, dt=F32):
        return nc.dram_tensor(name, list(shape), dt, kind="ExternalInput").ap()

    def dscr(name, shape, dt):
        kind = "ExternalOutput" if (dbg and name in dbg) else "Internal"
        return nc.dram_tensor(name, list(shape), dt, kind=kind).ap()

    x_in = din("x", [NLAT, D])
    ctx_in = din("ctx", [NCTX, D])
    ccT_in = din("ccT", [128, 8, 2])
    w_mod = din("w_mod", [L, D, 3 * D])
    b_mod = din("b_mod", [L, 3 * D])
    norm_w = din("norm_w", [L, D])
    w_in_p = din("w_in_p", [L, D, NCOLS])
    wq_up_p = din("wq_up_p", [L, 256, 2, 384])
    wkv_k = din("wkv_k", [L, 128, 256])
    wkv_v = din("wkv_v", [L, 128, 256])
    colv_in = din("colv", [L, 128, NV])
    alog_in = din("a_log", [L, 8])
    dtb_in = din("dt_bias", [L, 8])
    sink_in = din("sink", [L, 4])
    w_out = din("w_out", [L, D, D])
    fnw_in = din("final_norm_w", [D])
    consts_in = din("consts", [128, 5, 128])
    cos64_in = din("cos64", [128, TT])
    sin64_in = din("sin64", [128, TT])
    cos96_in = din("cos96", [128, TT])
    sin96_in = din("sin96", [128, TT])
    out_d = nc.dram_tensor("out", [NLAT, D], F32, kind="ExternalOutput").ap()

    mod_d = dscr("mod_d", [L, 2, 3 * D], F32)
    xres_d = dscr("xres_d", [TT, D], F32)
    qA_d = dscr("qA_d", [4, 96, TT], BF16)
    kA_d = dscr("kA_d", [4, 64, TT], BF16)
    kr_d = dscr("kr_d", [32, TT], BF16)
    VA_d = dscr("VA_d", [TT, 256], BF16)
    gA_d = dscr("gA_d", [256, TT], BF16)
    qB_d = dscr("qB_d", [2, 128, TT], BF16)
    kB_d = dscr("kB_d", [128, TT], BF16)
    VB_d = dscr("VB_d", [TT, 128], BF16)
    gB_d = dscr("gB_d", [256, TT], BF16)
    qC_d = dscr("qC_d", [2, 128, TT], BF16)
    kC_d = dscr("kC_d", [128, TT], BF16)
    VC_d = dscr("VC_d", [TT, 128], BF16)
    gC_d = dscr("gC_d", [256, TT], BF16)
    zs_d = dscr("zs_d", [256, TT], BF16)
    xbc_d = dscr("xbc_d", [768, TT], F32)
    dt_d = dscr("dt_d", [TT, 8], F32)
    ycat_d = dscr("ycat_d", [NT, 128, 8, 128], BF16)

    def ycat_rows(r0, nrows, t0, wd):
        k, p0 = r0 // 128, r0 % 128
        return ycat_d[t0 // 128:(t0 + wd) // 128, p0:p0 + nrows, k, :].rearrange("t p c -> p t c")

    dbgu_d = dscr("dbgu_d", [768, TT], BF16)
    dbgy_d = dscr("dbgy_d", [256, TT], F32)

    with contextlib.ExitStack() as es:
        fw = FW(nc, es)
        op, ld, st = fw.op, fw.ld, fw.st

        consts = fw.sb(es, "consts", [128, 5, 128], F32)
        cbf = fw.sb(es, "cbf", [128, 5, 128], BF16)
        epsb = fw.sb(es, "epsb", [128, 1], F32)
        ld(consts, consts[:], consts_in[:, :, :])
        op("dve", [consts], [cbf], lambda: V.tensor_copy(cbf[:], consts[:]))
        op("pool", [], [epsb], lambda: G.memset(epsb[:], EPS))
        P = [fw.ps(es, f"P{i}", [128, 512], F32) for i in range(8)]

        def rstd_from(ps_res, ps_ap, out_res, out_ap, n):
            op("act", [ps_res, epsb], [out_res], lambda: A.activation(out_ap, ps_ap, AF.Ln, bias=epsb[0:ps_ap.shape[0], 0:1], scale=1.0 / n))
            op("act", [out_res], [out_res], lambda: A.activation(out_ap, out_ap, AF.Exp, scale=-0.5))

        with contextlib.ExitStack() as s0:
            cc = fw.sb(s0, "cc", [128, 8, 2], F32)
            ce = fw.sb(s0, "ce", [128, 8, 2], F32)
            sc = fw.sb(s0, "sc", [128, 8, 2], F32)
            wm = [fw.sb(s0, f"wm{i}", [128, 8, 512], F32) for i in range(2)]
            bm = [fw.sb(s0, f"bm{i}", [2, 512], F32) for i in range(2)]
            mo = [fw.sb(s0, f"mo{i}", [2, 512], F32) for i in range(2)]
            ld(cc, cc[:], ccT_in[:, :, :])
            op("act", [cc], [ce], lambda: A.activation(ce[:], cc[:], AF.Exp, scale=-1.0))
            op("dve", [ce], [ce], lambda: V.tensor_scalar_add(ce[:], ce[:], 1.0))
            op("dve", [ce], [ce], lambda: V.reciprocal(ce[:], ce[:]))
            op("dve", [cc, ce], [sc], lambda: V.tensor_tensor(sc[:], cc[:], ce[:], ALU.mult))
            it = 0
            for l in range(nlayers):
                for cg in range(6):
                    b = it % 2
                    it += 1
                    ld(wm[b], wm[b][:], w_mod[l, :, cg * 512:(cg + 1) * 512].rearrange("(k p) c -> p k c", p=128), q=("sp" if b == 0 else "act"))
                    ld(bm[b], bm[b][:], b_mod[l, cg * 512:(cg + 1) * 512].partition_broadcast(2))
                    pm = P[b]
                    for k in range(8):
                        op("pe", [sc, wm[b]], [pm], lambda k=k, b=b, pm=pm: T.matmul(pm[0:2, :], sc[:, k, :], wm[b][:, k, :], start=(k == 0), stop=(k == 7)))
                    op("dve", [pm, bm[b]], [mo[b]], lambda b=b, pm=pm: V.tensor_tensor(mo[b][:], pm[0:2, :], bm[b][:], ALU.add))
                    st(mod_d[l, :, cg * 512:(cg + 1) * 512], mo[b], mo[b][:])
            fw.barrier()
        def chk(tag):
            if stop == tag:
                fw.stopped = True
        try:
          chk("S0")
          for l in range(nlayers):
              last = (l == L - 1)
              with contextlib.ExitStack() as sl:
                  colv = fw.sb(sl, "colv", [128, NV], F32)
                  ld(colv, colv[:], colv_in[l, :, :])
                  with contextlib.ExitStack() as s12:
                      hT = fw.sb(s12, "hT", [128, 8, TT], BF16)
                      with contextlib.ExitStack() as s1:
                          gmul = [fw.sb(s1, f"gmul{j}", [128, D], F32) for j in range(2)]
                          shf = [fw.sb(s1, f"shf{j}", [128, D], F32) for j in range(2)]
                          nwb = fw.sb(s1, "nwb", [128, D], F32)
                          tb = fw.sb(s1, "tb", [128, D], F32)
                          xt = [fw.sb(s1, f"xt{i}", [128, D], F32) for i in range(3)]
                          junk = fw.sb(s1, "junk", [128, D], BF16)
                          h1 = [fw.sb(s1, f"h1{i}", [128, D], F32) for i in range(3)]
                          hb = [fw.sb(s1, f"hb{i}", [128, D], BF16) for i in range(3)]
                          ss = [fw.sb(s1, f"ss{i}", [128, 1], F32) for i in range(3)]
                          ld(nwb, nwb[:], norm_w[l, :].partition_broadcast(128))
                          for j in range(2):
                              ld(tb, tb[:], mod_d[l, j, D:2 * D].partition_broadcast(128))
                              op("dve", [tb, nwb], [gmul[j]], lambda j=j: V.scalar_tensor_tensor(gmul[j][:], tb[:], 1.0, nwb[:], ALU.add, ALU.mult))
                              ld(shf[j], shf[j][:], mod_d[l, j, 0:D].partition_broadcast(128))
                          def s1_tile(t):
                              j = 1 if t < 2 else 0
                              if l == 0:
                                  src = ctx_in[t * 128:(t + 1) * 128, :] if t < 2 else x_in[(t - 2) * 128:(t - 1) * 128, :]
                              else:
                                  src = xres_d[t * 128:(t + 1) * 128, :]
                              X, S_, H1, HB, pt = xt[t % 3], ss[t % 3], h1[t % 3], hb[t % 3], P[t % 3]
                              ld(X, X[:], src)
                              op("pool", [], [S_], lambda: G.memset(S_[:], 0.0))
                              yield
                              op("act", [X], [junk, S_], lambda: A.activation(junk[:], X[:], AF.Square, accum_out=S_[:, 0:1]))
                              yield
                              op("act", [S_, epsb], [S_], lambda: A.activation(S_[:, 0:1], S_[:, 0:1], AF.Ln, bias=epsb[:, 0:1], scale=1.0 / D))
                              yield
                              op("act", [S_], [S_], lambda: A.activation(S_[:, 0:1], S_[:, 0:1], AF.Exp, scale=-0.5))
                              yield
                              op("dve", [X, S_, gmul[j]], [H1], lambda: V.scalar_tensor_tensor(H1[:], X[:], S_[:, 0:1], gmul[j][:], ALU.mult, ALU.mult))
                              yield
                              op("dve", [H1, shf[j]], [HB], lambda: V.tensor_tensor(HB[:], H1[:], shf[j][:], ALU.add))
                              yield
                              ptb = pt.t.bitcast(BF16)
                              for k in range(8):
                                  op("pe", [HB, cbf], [pt], lambda k=k: T.transpose(ptb[:, k * 128:(k + 1) * 128], HB[:, k * 128:(k + 1) * 128], cbf[:, 0, :]))
                              yield
                              op("act", [pt], [hT], lambda: A.copy(hT[:, :, t * 128:(t + 1) * 128], ptb[:, 0:1024].rearrange("p (k c) -> p k c", k=8)))
                              yield

                          run_window((s1_tile(t) for t in range(NT)), 3)
                          fw.barrier()
                      chk("S1")

                      with contextlib.ExitStack() as s2:
                          wst = [fw.sb(s2, f"wst{i}", [128, 8, 128], F32) for i in range(2)]
                          wb = [fw.sb(s2, f"wb{i}", [128, 8, 128], BF16) for i in range(8)]
                          wqs = fw.sb(s2, "wqs", [128, 2, 768], F32)
                          wq = fw.sb(s2, "wq", [128, 2, 768], BF16)
                          wks = fw.sb(s2, "wks", [128, 512], F32)
                          wk = fw.sb(s2, "wk", [128, 512], BF16)
                          tab = [[fw.sb(s2, f"tab{i}_{j}", [128, 512], F32) for j in range(2)] for i in range(3)]
                          sq_ = [fw.sb(s2, f"sq{i}", [128, 2, 512], BF16) for i in range(2)]
                          rs_ = [fw.sb(s2, f"rs{i}", [128, 512], F32) for i in range(2)]
                          cqn_ = [fw.sb(s2, f"cqn{i}", [128, 2, 512], BF16) for i in range(2)]
                          t1 = [fw.sb(s2, f"t1{i}", [128, 512], F32) for i in range(2)]
                          t2 = [fw.sb(s2, f"t2{i}", [128, 512], F32) for i in range(2)]
                          t3 = [fw.sb(s2, f"t3{i}", [128, 512], F32) for i in range(2)]
                          ob = [fw.sb(s2, f"ob{i}", [128, 512], BF16) for i in range(4)]
                          of = [fw.sb(s2, f"of{i}", [128, 512], F32) for i in range(2)]
                          cnt = {"wst": 0, "wb": 0, "ob": 0, "t": 0, "of": 0, "p": 0, "tab": 0}

                          def nxt(key, n):
                              v = cnt[key] % n
                              cnt[key] += 1
                              return v

                          def load_blocks(blks):
                              res = []
                              for bk in blks:
                                  ws = wst[nxt("wst", 2)]
                                  w = wb[nxt("wb", 8)]
                                  ld(ws, ws[:], w_in_p[l, :, bk * 128:(bk + 1) * 128].rearrange("(k p) c -> p k c", p=128))
                                  op("act", [ws], [w], lambda ws=ws, w=w: A.copy(w[:], ws[:]))
                                  res.append(w)
                              return res

                          def proj(w, t0, wd, pr):
                              for k in range(8):
                                  op("pe", [hT, w], [pr], lambda k=k: T.matmul(pr[:, 0:wd], w[:, k, :], hT[:, k, t0:t0 + wd], start=(k == 0), stop=(k == 7)))

                          def nextP():
                              return P[nxt("p", 8)]

                          def load_tabs(t0, wd, c_in, s_in):
                              tb_ = tab[nxt("tab", 3)]
                              ld(tb_[0], tb_[0][:, 0:wd], c_in[:, t0:t0 + wd])
                              ld(tb_[1], tb_[1][:, 0:wd], s_in[:, t0:t0 + wd])
                              return tb_

                          def rope_g(pa, pb, tb_, r0, r1, wd, wcol, wcolp, rstd, dst_ap):
                              i = nxt("t", 2)
                              a1, a2, a3 = t1[i], t2[i], t3[i]
                              o = ob[nxt("ob", 4)]
                              if wcol is None:
                                  op("dve", [pa, tb_[0]], [a1], lambda: V.tensor_tensor(a1[r0:r1, 0:wd], pa[r0:r1, 0:wd], tb_[0][r0:r1, 0:wd], ALU.mult))
                                  op("dve", [pb, tb_[1]], [a2], lambda: V.tensor_tensor(a2[r0:r1, 0:wd], pb[r0:r1, 0:wd], tb_[1][r0:r1, 0:wd], ALU.mult))
                              else:
                                  op("dve", [pa, tb_[0], colv], [a1], lambda: V.scalar_tensor_tensor(a1[r0:r1, 0:wd], pa[r0:r1, 0:wd], colv[r0:r1, wcol:wcol + 1], tb_[0][r0:r1, 0:wd], ALU.mult, ALU.mult))
                                  op("dve", [pb, tb_[1], colv], [a2], lambda: V.scalar_tensor_tensor(a2[r0:r1, 0:wd], pb[r0:r1, 0:wd], colv[r0:r1, wcolp:wcolp + 1], tb_[1][r0:r1, 0:wd], ALU.mult, ALU.mult))
                              yield
                              if rstd is None:
                                  op("pool", [a1, a2], [o], lambda: G.tensor_tensor(o[r0:r1, 0:wd], a1[r0:r1, 0:wd], a2[r0:r1, 0:wd], ALU.add))
                              else:
                                  op("pool", [a1, a2], [a3], lambda: G.tensor_tensor(a3[r0:r1, 0:wd], a1[r0:r1, 0:wd], a2[r0:r1, 0:wd], ALU.add))
                                  yield
                                  op("dve", [a3, rstd], [o], lambda: V.tensor_tensor(o[r0:r1, 0:wd], a3[r0:r1, 0:wd], rstd[r0:r1, 0:wd], ALU.mult))
                              st(dst_ap, o, o[r0:r1, 0:wd], q=SQ)
                              yield

                          def sumsq_g(srcs, ones_idx, n, wd, slot):
                              sq, rs = sq_[slot], rs_[slot]
                              for j, pr in enumerate(srcs):
                                  op("act", [pr], [sq], lambda j=j, pr=pr: A.activation(sq[:, j, 0:wd], pr[:, 0:wd], AF.Square))
                              yield
                              pss = nextP()
                              for j in range(len(srcs)):
                                  op("pe", [sq, cbf], [pss], lambda j=j: T.matmul(pss[:, 0:wd], cbf[:, ones_idx, :], sq[:, j, 0:wd], start=(j == 0), stop=(j == len(srcs) - 1)))
                              yield
                              op("act", [pss, epsb], [rs], lambda: A.activation(rs[:, 0:wd], pss[:, 0:wd], AF.Ln, bias=epsb[:, 0:1], scale=1.0 / n))
                              yield
                              op("act", [rs], [rs], lambda: A.activation(rs[:, 0:wd], rs[:, 0:wd], AF.Exp, scale=-0.5))
                              yield

                          ld(wqs, wqs[:], wq_up_p[l, :, :, :].rearrange("(k p) v c -> p k (v c)", p=128))
                          op("pool", [wqs], [wq], lambda: G.tensor_copy(wq[:], wqs[:]))
                          ld(wks, wks[:, 0:256], wkv_k[l, :, :])
                          ld(wks, wks[:, 256:512], wkv_v[l, :, :])
                          op("pool", [wks], [wk], lambda: G.tensor_copy(wk[:], wks[:]))
                          wcq = load_blocks([0, 1])

                          def cq_group(t0, wd, slot):
                              cqn, rs = cqn_[slot], rs_[slot]
                              tb_ = load_tabs(t0, wd, cos96_in, sin96_in)
                              pa, pb = nextP(), nextP()
                              proj(wcq[0], t0, wd, pa)
                              proj(wcq[1], t0, wd, pb)
                              yield
                              yield from sumsq_g([pa, pb], 3, 256, wd, slot)
                              for j, pr in enumerate((pa, pb)):
                                  op("dve", [pr, colv, rs], [cqn], lambda j=j, pr=pr: V.scalar_tensor_tensor(cqn[:, j, 0:wd], pr[:, 0:wd], colv[:, 4 + j:5 + j], rs[:, 0:wd], ALU.mult, ALU.mult))
                              yield
                              for h in range(4):
                                  po, pp = nextP(), nextP()
                                  for v, pr in enumerate((po, pp)):
                                      for k in range(2):
                                          op("pe", [wq, cqn], [pr], lambda v=v, k=k, pr=pr, h=h: T.matmul(pr[0:96, 0:wd], wq[:, k, v * 384 + h * 96: v * 384 + (h + 1) * 96], cqn[:, k, 0:wd], start=(k == 0), stop=(k == 1)))
                                  yield
                                  yield from rope_g(po, pp, tb_, 0, 96, wd, None, None, None, qA_d[h, :, t0:t0 + wd])

                          run_window((cq_group(t0, wd, gi % 2) for gi, (t0, wd) in enumerate(GROUPS)), 2)
                          chk("S2.q")
                          wckv = load_blocks([2, 3, 4])

                          def ckv_group(t0, wd, slot):
                              cqn, rs = cqn_[slot], rs_[slot]
                              tb_ = load_tabs(t0, wd, cos96_in, sin96_in)
                              pa = nextP()
                              proj(wckv[0], t0, wd, pa)
                              pra, prb = nextP(), nextP()
                              proj(wckv[1], t0, wd, pra)
                              proj(wckv[2], t0, wd, prb)
                              yield
                              yield from sumsq_g([pa], 3, 128, wd, slot)
                              op("dve", [pa, colv, rs], [cqn], lambda: V.scalar_tensor_tensor(cqn[:, 0, 0:wd], pa[:, 0:wd], colv[:, 6:7], rs[:, 0:wd], ALU.mult, ALU.mult))
                              yield
                              yield from rope_g(pra, prb, tb_, 64, 96, wd, None, None, None, kr_d[:, t0:t0 + wd])
                              for h in range(4):
                                  pk = nextP()
                                  op("pe", [wk, cqn], [pk], lambda pk=pk, h=h: T.matmul(pk[0:64, 0:wd], wk[:, h * 64:(h + 1) * 64], cqn[:, 0, 0:wd], start=True, stop=True))
                                  yield
                                  o = ob[nxt("ob", 4)]
                                  op("act", [pk], [o], lambda pk=pk, o=o: A.copy(o[0:64, 0:wd], pk[0:64, 0:wd]))
                                  st(kA_d[h, :, t0:t0 + wd], o, o[0:64, 0:wd], q=SQ)
                                  yield
                              for tt in range(wd // 128):
                                  pv = nextP()
                                  op("pe", [wk, cqn], [pv], lambda pv=pv, tt=tt: T.matmul(pv[:, 0:256], cqn[:, 0, tt * 128:(tt + 1) * 128], wk[:, 256:512], start=True, stop=True))
                                  yield
                                  o = ob[nxt("ob", 4)]
                                  op("act", [pv], [o], lambda pv=pv, o=o: A.copy(o[:, 0:256], pv[:, 0:256]))
                                  st(VA_d[t0 + tt * 128:t0 + (tt + 1) * 128, :], o, o[:, 0:256], q=SQ)
                                  yield

                          run_window((ckv_group(t0, wd, gi % 2) for gi, (t0, wd) in enumerate(GROUPS)), 2)

                          chk("S2.kv")
                          def gate_unit(blks, dst):
                              ws = load_blocks(blks)
                              for (t0, wd) in GROUPS:
                                  for j, w in enumerate(ws):
                                      pa = nextP()
                                      proj(w, t0, wd, pa)
                                      o = ob[nxt("ob", 4)]
                                      op("act", [pa], [o], lambda pa=pa, o=o: A.activation(o[:, 0:wd], pa[:, 0:wd], AF.Silu))
                                      st(dst[j * 128:(j + 1) * 128, t0:t0 + wd], o, o[:, 0:wd], q=SQ)

                          gate_unit([5, 6], gA_d)
                          chk("S2.gA")

                          def qk_unit(blk_o, blk_p, dsts, norm_cols):
                              ws = load_blocks(list(blk_o) + list(blk_p))
                              n = len(blk_o)

                              def qk_group(t0, wd, slot):
                                  tb_ = load_tabs(t0, wd, cos64_in, sin64_in)
                                  for j in range(n):
                                      pa, pb = nextP(), nextP()
                                      proj(ws[j], t0, wd, pa)
                                      proj(ws[n + j], t0, wd, pb)
                                      yield
                                      if norm_cols is not None:
                                          yield from sumsq_g([pa], 4, 64, wd, slot)
                                          yield from rope_g(pa, pb, tb_, 0, 128, wd, norm_cols[0], norm_cols[1], rs_[slot], dsts[j][:, t0:t0 + wd])
                                      else:
                                          yield from rope_g(pa, pb, tb_, 0, 128, wd, None, None, None, dsts[j][:, t0:t0 + wd])

                              run_window((qk_group(t0, wd, gi % 2) for gi, (t0, wd) in enumerate(GROUPS)), 2)

                          def v_unit(blk, dst):
                              (w,) = load_blocks([blk])
                              for t in range(NT):
                                  pv = nextP()
                                  for k in range(8):
                                      op("pe", [hT, w], [pv], lambda k=k, pv=pv, t=t: T.matmul(pv[:, 0:128], hT[:, k, t * 128:(t + 1) * 128], w[:, k, :], start=(k == 0), stop=(k == 7)))
                                  o = ob[nxt("ob", 4)]
                                  op("act", [pv], [o], lambda pv=pv, o=o: A.copy(o[:, 0:128], pv[:, 0:128]))
                                  st(dst[t * 128:(t + 1) * 128, :], o, o[:, 0:128], q=SQ)

                          def vdt_unit():
                              wc = fw.sb(s2, "wcomb", [128, 8, 264], BF16)
                              for (bk, c0, wn) in ((13, 0, 128), (22, 128, 128), (33, 256, 8)):
                                  ws = wst[nxt("wst", 2)]
                                  ld(ws, ws[:], w_in_p[l, :, bk * 128:(bk + 1) * 128].rearrange("(k p) c -> p k c", p=128))
                                  op("act", [ws], [wc], lambda ws=ws, c0=c0, wn=wn: A.copy(wc[:, :, c0:c0 + wn], ws[:, :, 0:wn]))
                              for t in range(NT):
                                  pv = nextP()
                                  for k in range(8):
                                      op("pe", [hT, wc], [pv], lambda k=k, pv=pv, t=t: T.matmul(pv[:, 0:264], hT[:, k, t * 128:(t + 1) * 128], wc[:, k, :], start=(k == 0), stop=(k == 7)))
                                  o = ob[nxt("ob", 4)]
                                  op("act", [pv], [o], lambda pv=pv, o=o: A.copy(o[:, 0:256], pv[:, 0:256]))
                                  st(VB_d[t * 128:(t + 1) * 128, :], o, o[:, 0:128], q=SQ)
                                  st(VC_d[t * 128:(t + 1) * 128, :], o, o[:, 128:256], q=SQ)
                                  o2 = of[nxt("of", 2)]
                                  op("act", [pv], [o2], lambda pv=pv, o2=o2: A.copy(o2[:, 0:8], pv[:, 256:264]))
                                  st(dt_d[t * 128:(t + 1) * 128, :], o2, o2[:, 0:8], q=SQ)

                          qk_unit([7, 8], [9, 10], [qB_d[0], qB_d[1]], (0, 1))
                          chk("S2.qB")
                          qk_unit([11], [12], [kB_d], (2, 3))
                          chk("S2.kB")
                          vdt_unit()
                          chk("S2.vB")
                          gate_unit([14, 15], gB_d)
                          qk_unit([16, 17], [18, 19], [qC_d[0], qC_d[1]], None)
                          qk_unit([20], [21], [kC_d], None)
                          gate_unit([23, 24], gC_d)
                          gate_unit([25, 26], zs_d)
                          chk("S2.z")
                          for half in range(2):
                              ws = load_blocks([27 + 3 * half + i for i in range(3)])
                              for (t0, wd) in GROUPS:
                                  for j, w in enumerate(ws):
                                      pa = nextP()
                                      proj(w, t0, wd, pa)
                                      o = of[nxt("of", 2)]
                                      op("act", [pa], [o], lambda pa=pa, o=o: A.copy(o[:, 0:wd], pa[:, 0:wd]))
                                      r0 = (3 * half + j) * 128
                                      st(xbc_d[r0:r0 + 128, t0:t0 + wd], o, o[:, 0:wd], q=SQ)
                          fw.barrier()
                      chk("S2")

                  def dense_attn(name, nkh, dq, kT_loads, v_src, head_info, y_row0, scale, padq=False):
                      with contextlib.ExitStack() as s3:
                          kT = fw.sb(s3, name + "kT", [128, nkh, TT], BF16)
                          Vt = fw.sb(s3, name + "V", [128, NT, nkh, 128], BF16)
                          qt = [fw.sb(s3, f"{name}q{i}", [128, 512], BF16) for i in range(6)]
                          gt = [fw.sb(s3, f"{name}g{i}", [128, 512], BF16) for i in range(3)]
                          pt = [fw.sb(s3, f"{name}p{i}", [128, 512], BF16) for i in range(4)]
                          if padq:
                              for i in range(6):
                                  op("pool", [], [qt[i]], lambda i=i: G.memset(qt[i][:], 0.0))
                              if dq == 96:
                                  op("pool", [], [kT], lambda: G.memset(kT[96:128, :, :], 0.0))
                          rd = [fw.sb(s3, f"{name}rd{i}", [128, 512], F32) for i in range(2)]
                          yf = [fw.sb(s3, f"{name}yf{i}", [128, 512], F32) for i in range(2)]
                          yb = [fw.sb(s3, f"{name}yb{i}", [128, 512], BF16) for i in range(3)]
                          for (r0, r1, i, src) in kT_loads:
                              ld(kT, kT[r0:r1, i, :], src)
                          op("pool", [], [Vt], lambda: G.memset(Vt[:], 1.0))
                          for i in range(nkh):
                              for b0 in range(0, NT, 9):
                                  b1 = min(NT, b0 + 9)
                                  ld(Vt, Vt[:, b0:b1, i, 0:64], v_src[b0 * 128:b1 * 128, i * 64:(i + 1) * 64].rearrange("(b p) f -> p b f", p=128))
                          items = [(gi, t0, wd, h) for gi, (t0, wd) in enumerate(GROUPS) for h in range(4)]

                          def issue_loads(idx):
                              gi, t0, wd, h = items[idx]
                              kh, pr0, vi, qap, gap = head_info(h, t0, wd)
                              qi = (idx % 3) + (3 if pr0 else 0)
                              ld(qt[qi], qt[qi][pr0:pr0 + dq, 0:wd], qap)
                              ld(gt[idx % 3], gt[idx % 3][0:64, 0:wd], gap)

                          LA = 2
                          steps = []
                          for c, (gi, t0, wd, h) in enumerate(items):
                              nkb = 2 if gi == 0 else NT
                              for kb in range(nkb):
                                  steps.append((c, kb, nkb))

                          def item_ctx(c):
                              gi, t0, wd, h = items[c]
                              kh, pr0, vi, qap, gap = head_info(h, t0, wd)
                              Q = qt[(c % 3) + (3 if pr0 else 0)]
                              ka, kb_ = (0, 128) if padq else (pr0, pr0 + dq)
                              return t0, wd, h, kh, vi, Q, gt[c % 3], P[4 + (c % 2)], ka, kb_

                          issue_loads(0)
                          for n in range(len(steps) + LA):
                              if n < len(steps):
                                  c, kb, nkb = steps[n]
                                  t0, wd, h, kh, vi, Q, Gt, O, ka, kb_ = item_ctx(c)
                                  if kb == 0 and c + 1 < len(items):
                                      issue_loads(c + 1)
                                  S = P[n % 4]
                                  Pt = pt[n % 4]
                                  op("pe", [kT, Q], [S], lambda: T.matmul(S[:, 0:wd], kT[ka:kb_, kh, kb * 128:(kb + 1) * 128], Q[ka:kb_, 0:wd], start=True, stop=True))
                                  op("act", [S], [Pt], lambda: A.activation(Pt[:, 0:wd], S[:, 0:wd], AF.Exp, scale=scale))
                              m = n - LA
                              if m >= 0:
                                  c, k2, nkb = steps[m]
                                  t0, wd, h, kh, vi, Q, Gt, O, ka, kb_ = item_ctx(c)
                                  Pt2 = pt[m % 4]
                                  op("pe", [Vt, Pt2], [O], lambda: T.matmul(O[:, 0:wd], Vt[:, k2, vi, :], Pt2[:, 0:wd], start=(k2 == 0), stop=(k2 == nkb - 1)))
                                  if k2 == nkb - 1:
                                      R, Yf, Yb = rd[c % 2], yf[c % 2], yb[c % 3]
                                      op("dve", [O], [R], lambda: V.reciprocal(R[0:64, 0:wd], O[64:128, 0:wd]))
                                      op("dve", [O, R], [Yf], lambda: V.tensor_tensor(Yf[0:64, 0:wd], O[0:64, 0:wd], R[0:64, 0:wd], ALU.mult))
                                      op("pool", [Yf, Gt], [Yb], lambda: G.tensor_tensor(Yb[0:64, 0:wd], Yf[0:64, 0:wd], Gt[0:64, 0:wd], ALU.mult))
                                      st(ycat_rows(y_row0 + h * 64, 64, t0, wd), Yb, Yb[0:64, 0:wd].rearrange("p (t c) -> p t c", c=128))
                          fw.barrier()
                      chk("S3" + name)

                  dense_attn("A", 4, 96,
                             [(0, 64, h, kA_d[h, :, :]) for h in range(4)] + [(64, 96, h, kr_d[:, :]) for h in range(4)],
                             VA_d,
                             lambda h, t0, wd: (h, 0, h, qA_d[h, :, t0:t0 + wd], gA_d[h * 64:(h + 1) * 64, t0:t0 + wd]),
                             0, 96 ** -0.5, padq=True)
                  dense_attn("B", 2, 64,
                             [(0, 128, 0, kB_d[:, :])],
                             VB_d,
                             lambda h, t0, wd: (0, (h // 2) * 64, h // 2, qB_d[h % 2, (h // 2) * 64:(h // 2 + 1) * 64, t0:t0 + wd], gB_d[h * 64:(h + 1) * 64, t0:t0 + wd]),
                             256, 0.125, padq=True)

                  with contextlib.ExitStack() as s3:
                      kT = fw.sb(s3, "CkT", [128, TT], BF16)
                      Vt = fw.sb(s3, "CV", [128, NT, 2, 128], BF16)
                      qt = [fw.sb(s3, f"Cq{i}", [128, 2, 128], BF16) for i in range(8)]
                      gt = [fw.sb(s3, f"Cg{i}", [128, 2, 128], BF16) for i in range(8)]
                      pt = [fw.sb(s3, f"Cp{i}", [128, 640], BF16) for i in range(8)]
                      rd = [fw.sb(s3, f"Crd{i}", [128, 128], F32) for i in range(8)]
                      yf = [fw.sb(s3, f"Cyf{i}", [128, 128], F32) for i in range(8)]
                      yb = [fw.sb(s3, f"Cyb{i}", [128, 128], BF16) for i in range(8)]
                      skb = fw.sb(s3, "skb", [1, 4], F32)
                      skr = fw.sb(s3, "skr", [1, 4, 128], BF16)
                      ld(kT, kT[:, :], kC_d[:, :])
                      op("pool", [], [Vt], lambda: G.memset(Vt[:], 1.0))
                      for g in range(2):
                          for b0 in range(0, NT, 9):
                              b1 = min(NT, b0 + 9)
                              ld(Vt, Vt[:, b0:b1, g, 0:64], VC_d[b0 * 128:b1 * 128, g * 64:(g + 1) * 64].rearrange("(b p) f -> p b f", p=128))
                      ld(skb, skb[:], sink_in[l, :].partition_broadcast(1))
                      op("act", [skb], [skb], lambda: A.activation(skb[:], skb[:], AF.Exp))
                      op("pool", [], [skr], lambda: G.memset(skr[:], 0.0))
                      op("dve", [skb], [skr], lambda: V.tensor_copy(skr[:, :, 64:128], skb[:, :].unsqueeze(2).to_broadcast([1, 4, 64])))
                      def c_block(qb, slot):
                          Q = qt[qb % 8]
                          Gt = gt[qb % 8]
                          ld(Q, Q[:, :, :], qC_d[:, :, qb * 128:(qb + 1) * 128].rearrange("j p t -> p j t"))
                          ld(Gt, Gt[:, :, :], gC_d[:, qb * 128:(qb + 1) * 128].rearrange("(j p) t -> p j t", p=128))
                          kbs = [(0, None), (1, None)]
                          if qb >= 2:
                              n = qb - 2
                              if n > 0:
                                  kbs.append((qb - 1, 2))
                              kbs.append((qb, None))
                              if n < 31:
                                  kbs.append((qb + 1, 1))
                          nk = len(kbs)
                          yield
                          for h in range(4):
                              g, j = h // 2, h % 2
                              Yb = yb[slot * 2 + g]
                              Sa = P[slot * 2]
                              SbO = P[slot * 2 + 1]
                              Pt = pt[slot * 2 + (h % 2)]
                              for i, (kb, _) in enumerate(kbs):
                                  Sr = Sa if i < 4 else SbO
                                  dst = Sa[:, i * 128:(i + 1) * 128] if i < 4 else SbO[:, 0:128]
                                  op("pe", [kT, Q], [Sr], lambda dst=dst, kb=kb, g=g, j=j: T.matmul(dst, kT[g * 64:(g + 1) * 64, kb * 128:(kb + 1) * 128], Q[g * 64:(g + 1) * 64, j, :], start=True, stop=True))
                              yield
                              na = min(nk, 4)
                              op("act", [Sa], [Pt], lambda: A.activation(Pt[:, 0:na * 128], Sa[:, 0:na * 128], AF.Exp, scale=0.125))
                              if nk > 4:
                                  op("act", [SbO], [Pt], lambda: A.activation(Pt[:, 512:640], SbO[:, 0:128], AF.Exp, scale=0.125))
                              yield
                              for i, (kb, m) in enumerate(kbs):
                                  if m is not None:
                                      op("pool", [Pt, cbf], [Pt], lambda i=i, m=m: G.tensor_tensor(Pt[:, i * 128:(i + 1) * 128], Pt[:, i * 128:(i + 1) * 128], cbf[:, m, :], ALU.mult))
                              yield
                              for i, (kb, _) in enumerate(kbs):
                                  op("pe", [Vt, Pt], [SbO], lambda i=i, kb=kb: T.matmul(SbO[:, 128:256], Vt[:, kb, g, :], Pt[:, i * 128:(i + 1) * 128], start=(i == 0), stop=False))
                              op("pe", [skr, cbf], [SbO], lambda: T.matmul(SbO[:, 128:256], skr[0:1, h, :], cbf[0:1, 3, :], start=False, stop=True))
                              yield
                              R, Yf = rd[slot * 2 + (h % 2)], yf[slot * 2 + (h % 2)]
                              op("dve", [SbO], [R], lambda: V.reciprocal(R[j * 64:(j + 1) * 64, :], SbO[64:128, 128:256]))
                              yield
                              op("dve", [SbO, R], [Yf], lambda: V.tensor_tensor(Yf[j * 64:(j + 1) * 64, :], SbO[0:64, 128:256], R[j * 64:(j + 1) * 64, :], ALU.mult))
                              yield
                              op("pool", [Yf, Gt], [Yb], lambda: G.tensor_tensor(Yb[j * 64:(j + 1) * 64, :], Yf[j * 64:(j + 1) * 64, :], Gt[j * 64:(j + 1) * 64, g, :], ALU.mult))
                              if j == 1:
                                  st(ycat_d[qb, :, 4 + g, :], Yb, Yb[:, :], q="act")
                              yield

                      run_window(((lambda slot, qb=qb: c_block(qb, slot)) for qb in range(NT)), 4)
                      fw.barrier()
                  chk("S3C")

                  with contextlib.ExitStack() as s4:
                      uT = fw.sb(s4, "uT", [128, 6, TT], BF16)
                      yacc = [fw.sb(s4, f"yacc{d}", [128, 2, TT], F32) for d in range(2)]
                      with contextlib.ExitStack() as s4a:
                          xin = [fw.sb(s4a, f"xin{i}", [128, 516], F32) for i in range(3)]
                          cva = [fw.sb(s4a, f"cva{i}", [128, 512], F32) for i in range(3)]
                          cve = [fw.sb(s4a, f"cve{i}", [128, 512], F32) for i in range(2)]
                          it = 0
                          for gi, (t0, wd) in enumerate(GROUPS):
                              s_lo, s_hi = (0, NCTX) if gi == 0 else (NCTX, TT)
                              lo, hi = t0 - 2, t0 + wd + 2
                              clo, chi = max(lo, s_lo), min(hi, s_hi)
                              for tl in range(6):
                                  X = xin[it % 3]
                                  acc = cva[it % 3]
                                  e_ = cve[it % 2]
                                  it += 1
                                  if clo > lo or chi < hi:
                                      op("pool", [], [X], lambda X=X: G.memset(X[:, 0:wd + 4], 0.0))
                                  ld(X, X[:, clo - lo:chi - lo], xbc_d[tl * 128:(tl + 1) * 128, clo:chi])
                                  cw = 7 + tl * 5
                                  op("dve", [X, colv], [acc], lambda X=X, acc=acc, cw=cw, tl=tl: V.tensor_scalar(acc[:, 0:wd], X[:, 0:wd], colv[:, cw:cw + 1], colv[:, 37 + tl:38 + tl], ALU.mult, ALU.add))
                                  for k in range(1, 5):
                                      eng = "dve"
                                      E_ = V
                                      op(eng, [X, colv, acc], [acc], lambda X=X, acc=acc, cw=cw, k=k, E_=E_: E_.scalar_tensor_tensor(acc[:, 0:wd], X[:, k:k + wd], colv[:, cw + k:cw + k + 1], acc[:, 0:wd], ALU.mult, ALU.add))
                                  op("act", [acc], [uT], lambda acc=acc, tl=tl: A.activation(uT[:, tl, t0:t0 + wd], acc[:, 0:wd], AF.Silu))
                          fw.barrier()
                      with contextlib.ExitStack() as s4b:
                          arow = fw.sb(s4b, "arow", [128, 8], F32)
                          brow = fw.sb(s4b, "brow", [128, 8], F32)
                          NSET = 2
                          Sst = [fw.sb(s4b, f"Sst{d}", [128, 4, 64], F32) for d in range(2)]
                          Spad = [fw.sb(s4b, f"Spad{d}", [128, 4, 128], BF16) for d in range(2)]
                          def mk(nm, shape, dt):
                              return [[fw.sb(s4b, f"{nm}{d}_{i}", shape, dt) for i in range(NSET)] for d in range(2)]
                          xdp = mk("xdp", [128, 4, 128], BF16)
                          xdw = mk("xdw", [128, 4, 64], BF16)
                          dtr = mk("dtr", [128, 4], F32)
                          sm = mk("sm", [128, 8, 4], F32)
                          cs = mk("cs", [128, 8], F32)
                          dAb = mk("dAb", [128, 4, 128], F32)
                          seg = mk("seg", [128, 4, 128], F32)
                          EA = mk("EA", [128, 4, 128], F32)
                          CBm = mk("CBm", [128, 2, 128], F32)
                          scr = mk("scr", [128, 4, 128], BF16)
                          cdc = mk("cdc", [128, 4, 128], BF16)
                          xtk = mk("xtk", [128, 512], BF16)
                          ld(arow, arow[:], alog_in[l, :].partition_broadcast(128))
                          ld(brow, brow[:], dtb_in[l, :].partition_broadcast(128))
                          op("act", [arow], [arow], lambda: A.activation(arow[:], arow[:], AF.Exp))
                          op("dve", [arow], [arow], lambda: V.tensor_scalar_mul(arow[:], arow[:], -1.0))
                          for d in range(2):
                              for i in range(NSET):
                                  op("pool", [], [xdp[d][i]], lambda d=d, i=i: G.memset(xdp[d][i][:], 0.0))
                              op("pool", [], [Sst[d]], lambda d=d: G.memset(Sst[d][:], 0.0))
                              op("pool", [], [Spad[d]], lambda d=d: G.memset(Spad[d][:], 0.0))
                          orders = [list(range(NT)), [1, 0] + list(range(NT - 1, 1, -1))]

                          def bview(ap2, n):
                              return ap2.unsqueeze(2)

                          def ssd_P(d, ci, ch):
                              tri = 1 if d == 0 else 2
                              i = ci % NSET
                              c0 = ch * 128
                              pA, pB, pC, pD = P[4 * d], P[4 * d + 1], P[4 * d + 2], P[4 * d + 3]
                              DT, SM, CS = dtr[d][i], sm[d][i], cs[d][i]
                              ld(DT, DT[:], dt_d[c0:c0 + 128, d * 4:(d + 1) * 4])
                              ptrb = pB.t.bitcast(BF16)
                              for q_ in range(4):
                                  op("pe", [uT, cbf], [pB], lambda q_=q_: T.transpose(ptrb[:, q_ * 128:(q_ + 1) * 128], uT[:, q_, c0:c0 + 128], cbf[:, 0, :]))
                              for g in range(2):
                                  op("pe", [uT], [pC], lambda g=g: T.matmul(pC[:, g * 128:(g + 1) * 128], uT[:, 2 + g, c0:c0 + 128], uT[:, 4 + g, c0:c0 + 128], start=True, stop=True))
                              yield
                              op("dve", [DT, brow], [SM], lambda: V.tensor_tensor(SM[:, 0, :], DT[:], brow[:, d * 4:(d + 1) * 4], ALU.add))
                              XT = xtk[d][i]
                              op("act", [pB], [XT], lambda: A.copy(XT[:, 0:512], ptrb[:, 0:512]))
                              yield
                              op("act", [SM], [SM], lambda: A.activation(SM[:, 1, :], SM[:, 0, :], AF.Abs))
                              CB = CBm[d][i]
                              op("dve", [pC, consts], [CB], lambda: V.tensor_tensor(CB[:], pC[:, 0:256].rearrange("p (g t) -> p g t", g=2), consts[:, tri, :].unsqueeze(1).to_broadcast([128, 2, 128]), ALU.mult))
                              yield
                              op("act", [SM], [SM], lambda: A.activation(SM[:, 1, :], SM[:, 1, :], AF.Exp, scale=-1.0))
                              yield
                              op("dve", [SM], [SM], lambda: V.tensor_scalar_add(SM[:, 1, :], SM[:, 1, :], 1.0))
                              yield
                              op("act", [SM], [SM], lambda: A.activation(SM[:, 2, :], SM[:, 1, :], AF.Ln))
                              yield
                              op("dve", [SM], [SM], lambda: V.scalar_tensor_tensor(SM[:, 3, :], SM[:, 0, :], 0.0, SM[:, 2, :], ALU.max, ALU.add))
                              yield
                              op("dve", [SM, arow], [SM], lambda: V.tensor_tensor(SM[:, 4, :], SM[:, 3, :], arow[:, d * 4:(d + 1) * 4], ALU.mult))
                              yield
                              op("pe", [consts, SM], [pB], lambda: T.matmul(pB[:, 256:260], consts[:, tri, :], SM[:, 4, :], start=True, stop=True))
                              op("pe", [consts, SM], [pB], lambda: T.matmul(pB[:, 260:264], consts[:, 3, :], SM[:, 4, :], start=True, stop=True))
                              DB = dAb[d][i]
                              op("dve", [consts, SM], [DB], lambda: V.tensor_tensor(DB[:], consts[:, 3, :].unsqueeze(1).to_broadcast([128, 4, 128]), SM[:, 4, :].unsqueeze(2).to_broadcast([128, 4, 128]), ALU.mult))
                              yield
                              op("act", [pB], [CS], lambda: A.copy(CS[:, 0:8], pB[:, 256:264]))
                              for h in range(4):
                                  op("pe", [DB, consts], [pA], lambda h=h: T.matmul(pA[:, h * 128:(h + 1) * 128], DB[:, h, :], consts[:, tri, :], start=True, stop=True))
                              yield
                              SG, EAi = seg[d][i], EA[d][i]
                              for h in range(4):
                                  op("dve", [pA, CS], [SG], lambda h=h: V.tensor_scalar(SG[:, h, :], pA[:, h * 128:(h + 1) * 128], CS[:, h:h + 1], 0.0, ALU.subtract, ALU.min))
                              yield
                              op("act", [pA], [EAi], lambda: A.activation(EAi[:].rearrange("p h t -> p (h t)"), pA[:, 0:512], AF.Exp))
                              op("dve", [CS], [SM], lambda: V.tensor_tensor(SM[:, 5, :], CS[:, 4:8], CS[:, 0:4], ALU.subtract))
                              yield
                              op("act", [SG], [SG], lambda: A.activation(SG[:], SG[:], AF.Exp))
                              yield
                              op("act", [SM], [SM], lambda: A.activation(SM[:, 5, :], SM[:, 5, :], AF.Exp))
                              op("act", [CS], [SM], lambda: A.activation(SM[:, 6, :], CS[:, 4:8], AF.Exp))
                              SC, CD = scr[d][i], cdc[d][i]
                              for g in range(2):
                                  op("pool", [EAi, uT], [CD], lambda g=g: G.tensor_tensor(CD[:, 2 * g:2 * g + 2, :], EAi[:, 2 * g:2 * g + 2, :], uT[:, 4 + g, c0:c0 + 128].unsqueeze(1).to_broadcast([128, 2, 128]), ALU.mult))
                              yield
                              for g in range(2):
                                  op("dve", [SG, CB], [SC], lambda g=g: V.tensor_tensor(SC[:, 2 * g:2 * g + 2, :], SG[:, 2 * g:2 * g + 2, :], CB[:, g, :].unsqueeze(1).to_broadcast([128, 2, 128]), ALU.mult))
                              op("dve", [SM], [SM], lambda: V.tensor_tensor(SM[:, 7, :], SM[:, 3, :], SM[:, 5, :], ALU.mult))
                              yield
                              XP, XW = xdp[d][i], xdw[d][i]
                              for j in range(2):
                                  op("dve", [XT, SM], [XP], lambda j=j: V.tensor_tensor(XP[:, j::2, j * 64:(j + 1) * 64], XT[:, 0:256].rearrange("p (h f) -> p h f", h=4)[:, j::2, :], SM[:, 3, j::2].unsqueeze(2).to_broadcast([128, 2, 64]), ALU.mult))
                              op("dve", [XT, SM], [XW], lambda: V.tensor_tensor(XW[:], XT[:, 0:256].rearrange("p (h f) -> p h f", h=4), SM[:, 7, :].unsqueeze(2).to_broadcast([128, 4, 64]), ALU.mult))
                              yield
                              for h in range(4):
                                  g = h // 2
                                  op("pe", [XT, XW], [pD], lambda h=h, g=g: T.matmul(pD[:, 256 + h * 64:256 + (h + 1) * 64], XT[:, 256 + g * 128:256 + (g + 1) * 128], XW[:, h, :], start=True, stop=True))
                              yield

                          def ssd_Q(d, ci, ch):
                              i = ci % NSET
                              c0 = ch * 128
                              pD = P[4 * d + 3]
                              SM = sm[d][i]
                              XP, SC, CD = xdp[d][i], scr[d][i], cdc[d][i]
                              for tl in range(2):
                                  for j in range(2):
                                      h = 2 * tl + j
                                      op("pe", [XP, SC], [pD], lambda h=h, tl=tl, j=j: T.matmul(pD[:, tl * 128:(tl + 1) * 128], XP[:, h, :], SC[:, h, :], start=(j == 0), stop=False))
                                      op("pe", [Spad[d], CD], [pD], lambda h=h, tl=tl, j=j: T.matmul(pD[:, tl * 128:(tl + 1) * 128], Spad[d][:, h, :], CD[:, h, :], start=False, stop=(j == 1)))
                              yield
                              op("dve", [Sst[d], SM], [Sst[d]], lambda: V.tensor_tensor(Sst[d][:], Sst[d][:], SM[:, 6, :].unsqueeze(2).to_broadcast([128, 4, 64]), ALU.mult))
                              yield
                              op("dve", [Sst[d], pD], [Sst[d]], lambda: V.tensor_tensor(Sst[d][:], Sst[d][:], pD[:, 256:512].rearrange("p (h f) -> p h f", h=4), ALU.add))
                              yield
                              if d == 0:
                                  for tl in range(2):
                                      op("dve", [uT, colv, pD], [yacc[0]], lambda tl=tl: V.scalar_tensor_tensor(yacc[0][:, tl, c0:c0 + 128], uT[:, tl, c0:c0 + 128], colv[:, 43 + tl:44 + tl], pD[:, tl * 128:(tl + 1) * 128], ALU.mult, ALU.add))
                              else:
                                  op("act", [pD], [yacc[1]], lambda: A.copy(yacc[1][:, :, c0:c0 + 128], pD[:, 0:256].rearrange("p (a t) -> p a t", a=2)))
                              for j in range(2):
                                  op("pool", [Sst[d]], [Spad[d]], lambda j=j: G.tensor_copy(Spad[d][:, j::2, j * 64:(j + 1) * 64], Sst[d][:, j::2, :]))
                              yield

                          run_chains([ssd_P(0, 0, orders[0][0]), ssd_P(1, 0, orders[1][0])])
                          for ci in range(NT):
                              gl = [ssd_Q(0, ci, orders[0][ci]), ssd_Q(1, ci, orders[1][ci])]
                              if ci + 1 < NT:
                                  gl += [ssd_P(0, ci + 1, orders[0][ci + 1]), ssd_P(1, ci + 1, orders[1][ci + 1])]
                              run_chains(gl)
                          fw.barrier()
                      if dbg and "dbgu_d" in dbg:
                          for tl in range(6):
                              st(dbgu_d[tl * 128:(tl + 1) * 128, :], uT, uT[:, tl, :])
                          for tl in range(2):
                              st(dbgy_d[tl * 128:(tl + 1) * 128, :], yacc[0], yacc[0][:, tl, :])
                      with contextlib.ExitStack() as s4c:
                          zt = [fw.sb(s4c, f"zt{i}", [128, 512], BF16) for i in range(2)]
                          yz = [fw.sb(s4c, f"yz{i}", [128, 512], F32) for i in range(2)]
                          sq2 = fw.sb(s4c, "sq2", [128, 2, 512], F32)
                          rs2 = fw.sb(s4c, "rs2", [128, 512], F32)
                          yo = [fw.sb(s4c, f"yo{i}", [128, 512], BF16) for i in range(2)]
                          for (t0, wd) in GROUPS:
                              for tl in range(2):
                                  ld(zt[tl], zt[tl][:, 0:wd], zs_d[tl * 128:(tl + 1) * 128, t0:t0 + wd])
                                  op("pool", [yacc[0], yacc[1]], [yz[tl]], lambda tl=tl: G.tensor_tensor(yz[tl][:, 0:wd], yacc[0][:, tl, t0:t0 + wd], yacc[1][:, tl, t0:t0 + wd], ALU.add))
                                  op("dve", [yz[tl], zt[tl]], [yz[tl]], lambda tl=tl: V.tensor_tensor(yz[tl][:, 0:wd], yz[tl][:, 0:wd], zt[tl][:, 0:wd], ALU.mult))
                                  op("act", [yz[tl]], [sq2], lambda tl=tl: A.activation(sq2[:, tl, 0:wd], yz[tl][:, 0:wd], AF.Square))
                              pss = P[6]
                              for tl in range(2):
                                  op("pe", [sq2, consts], [pss], lambda tl=tl: T.matmul(pss[:, 0:wd], consts[:, 3, :], sq2[:, tl, 0:wd], start=(tl == 0), stop=(tl == 1)))
                              rstd_from(pss, pss[:, 0:wd], rs2, rs2[:, 0:wd], 256)
                              for tl in range(2):
                                  op("dve", [yz[tl], colv, rs2], [yo[tl]], lambda tl=tl: V.scalar_tensor_tensor(yo[tl][:, 0:wd], yz[tl][:, 0:wd], colv[:, 45 + tl:46 + tl], rs2[:, 0:wd], ALU.mult, ALU.mult))
                                  st(ycat_rows(768 + tl * 128, 128, t0, wd), yo[tl], yo[tl][:, 0:wd].rearrange("p (t c) -> p t c", c=128))
                      fw.barrier()
                  chk("S4")

                  with contextlib.ExitStack() as s5:
                      wos = fw.sb(s5, "wos", [128, 8, 512], F32)
                      wo = fw.sb(s5, "wo", [128, 8, D], BF16)
                      gb = [fw.sb(s5, f"gb{j}", [128, D], F32) for j in range(2)]
                      fnw = fw.sb(s5, "fnw", [128, D], F32)
                      yc = [fw.sb(s5, f"yc{i}", [128, 8, 128], BF16) for i in range(3)]
                      xs = [fw.sb(s5, f"xs{i}", [128, D], F32) for i in range(3)]
                      tm = [fw.sb(s5, f"tm{i}", [128, D], F32) for i in range(3)]
                      xn = [fw.sb(s5, f"xn{i}", [128, D], F32) for i in range(3)]
                      junk5 = fw.sb(s5, "junk5", [128, D], BF16)
                      s5s = [fw.sb(s5, f"s5s{i}", [128, 1], F32) for i in range(3)]
                      for half in range(2):
                          ld(wos, wos[:], w_out[l, :, half * 512:(half + 1) * 512].rearrange("(k p) c -> p k c", p=128))
                          op("pool", [wos], [wo], lambda half=half: G.tensor_copy(wo[:, :, half * 512:(half + 1) * 512], wos[:]))
                      for j in range(2):
                          ld(gb[j], gb[j][:], mod_d[l, j, 2 * D:3 * D].partition_broadcast(128))
                      if last:
                          ld(fnw, fnw[:], fnw_in.partition_broadcast(128))
                      tiles5 = [t for t in range(NT) if not (last and t < 2)]

                      def loads5(t):
                          YC, X = yc[t % 3], xs[t % 3]
                          ld(YC, YC[:], ycat_d[t, :, :, :])
                          if l == 0:
                              src = ctx_in[t * 128:(t + 1) * 128, :] if t < 2 else x_in[(t - 2) * 128:(t - 1) * 128, :]
                          else:
                              src = xres_d[t * 128:(t + 1) * 128, :]
                          ld(X, X[:], src, q="pool")

                      def s5_tile(t):
                          j = 1 if t < 2 else 0
                          YC, X, TM, XN = yc[t % 3], xs[t % 3], tm[t % 3], xn[t % 3]
                          loads5(t)
                          yield
                          pos = [P[(2 * (t % 3)) + half] for half in range(2)]
                          for half in range(2):
                              po = pos[half]
                              for k in range(8):
                                  op("pe", [YC, wo], [po], lambda k=k, half=half, po=po: T.matmul(po[:, 0:512], YC[:, k, :], wo[:, k, half * 512:(half + 1) * 512], start=(k == 0), stop=(k == 7)))
                          yield
                          for half in range(2):
                              po = pos[half]
                              op("dve", [po, gb[j]], [TM], lambda half=half, po=po: V.tensor_tensor(TM[:, half * 512:(half + 1) * 512], po[:, 0:512], gb[j][:, half * 512:(half + 1) * 512], ALU.mult))
                          yield
                          op("dve", [TM, X], [XN], lambda: V.tensor_tensor(XN[:], TM[:], X[:], ALU.add))
                          yield
                          if not last:
                              st(xres_d[t * 128:(t + 1) * 128, :], XN, XN[:], q="act")
                          else:
                              S_ = s5s[t % 3]
                              op("pool", [], [S_], lambda: G.memset(S_[:], 0.0))
                              op("act", [XN], [junk5, S_], lambda: A.activation(junk5[:], XN[:], AF.Square, accum_out=S_[:, 0:1]))
                              yield
                              rstd_from(S_, S_[:, 0:1], S_, S_[:, 0:1], D)
                              yield
                              op("dve", [XN, S_, fnw], [TM], lambda: V.scalar_tensor_tensor(TM[:], XN[:], S_[:, 0:1], fnw[:], ALU.mult, ALU.mult))
                              st(out_d[(t - 2) * 128:(t - 1) * 128, :], TM, TM[:], q="act")
                          yield

                      run_window((s5_tile(t) for t in tiles5), 3)
                      fw.barrier()
        except _Stop:
            pass
        fw.barrier(force=True)
        print("instructions:", fw.ninstr, "sems:", fw.nsem)
    return nc


def _tables():
    f32 = np.float32
    n = np.arange(NLAT)
    row = (n // 64).astype(f32)
    col = (n % 64).astype(f32)

    def ang(rot_dim):
        nf = rot_dim // 4
        inv = (f32(10000.0) ** (-(np.arange(nf, dtype=f32) / f32(nf)))).astype(f32)
        return np.concatenate([row[:, None] * inv, col[:, None] * inv], axis=-1).astype(f32)

    a64 = ang(64)
    a32 = ang(32)
    cos64 = np.ones((128, TT), f32)
    sin64 = np.zeros((128, TT), f32)
    for r in range(128):
        d = r % 64
        cos64[r, NCTX:] = np.cos(a64[:, d % 32])
        sin64[r, NCTX:] = (-1.0 if d < 32 else 1.0) * np.sin(a64[:, d % 32])
    cos96 = np.ones((128, TT), f32)
    sin96 = np.zeros((128, TT), f32)
    for r in range(64, 96):
        d = r - 64
        cos96[r, NCTX:] = np.cos(a32[:, d % 16])
        sin96[r, NCTX:] = (-1.0 if d < 16 else 1.0) * np.sin(a32[:, d % 16])
    cos96[96:] = 0.0
    k = np.arange(128)
    consts = np.zeros((128, 5, 128), f32)
    consts[:, 0] = np.eye(128, dtype=f32)
    consts[:, 1] = (k[:, None] <= k[None, :]).astype(f32)
    consts[:, 2] = (k[:, None] >= k[None, :]).astype(f32)
    consts[:, 3] = 1.0
    consts[:, 4] = ((k[:, None] // 64) == (k[None, :] // 64)).astype(f32)
    return cos64, sin64, cos96, sin96, consts


def _col_index():
    p64 = [(d + 32) % 64 for d in range(64)]
    p32 = [(d + 16) % 32 for d in range(32)]
    blocks = []
    blocks.append(list(range(0, 128)))
    blocks.append(list(range(128, 256)))
    blocks.append(list(range(256, 384)))
    blocks.append([-1] * 64 + [384 + d for d in range(32)] + [-1] * 32)
    blocks.append([-1] * 64 + [384 + p32[d] for d in range(32)] + [-1] * 32)
    blocks.append(list(range(416, 544)))
    blocks.append(list(range(544, 672)))
    for base in (672, 1440):
        q0, k0, v0, g0 = base, base + 256, base + 384, base + 512
        for j in range(2):
            blocks.append([q0 + j * 64 + d for d in range(64)] + [q0 + (2 + j) * 64 + d for d in range(64)])
        for j in range(2):
            blocks.append([q0 + j * 64 + p64[d] for d in range(64)] + [q0 + (2 + j) * 64 + p64[d] for d in range(64)])
        blocks.append(list(range(k0, k0 + 128)))
        blocks.append([k0 + g * 64 + p64[d] for g in range(2) for d in range(64)])
        blocks.append(list(range(v0, v0 + 128)))
        blocks.append(list(range(g0, g0 + 128)))
        blocks.append(list(range(g0 + 128, g0 + 256)))
    blocks.append(list(range(2208, 2336)))
    blocks.append(list(range(2336, 2464)))
    for i in range(6):
        blocks.append(list(range(2464 + i * 128, 2464 + (i + 1) * 128)))
    blocks.append(list(range(3232, 3240)) + [-1] * 120)
    assert len(blocks) == NBLK
    return np.array([c for b in blocks for c in b], dtype=np.int64)


_CACHE = {}


def _prep_shared(inp):
    f32 = np.float32
    idx = _col_index()
    w_in = np.asarray(inp["w_in"], f32)
    w_in_z = np.concatenate([w_in, np.zeros((L, D, 1), f32)], axis=-1)
    w_in_p = np.ascontiguousarray(w_in_z[:, :, np.where(idx < 0, 3240, idx)])
    wq = np.asarray(inp["mla_w_q_up"], f32)
    pA = np.array([h * 96 + (d if d < 64 else 64 + (d - 64 + 16) % 32) for h in range(4) for d in range(96)])
    wq_up_p = np.ascontiguousarray(np.stack([wq, wq[:, :, pA]], axis=2))
    wkv = np.asarray(inp["mla_w_kv_up"], f32).reshape(L, 128, 4, 128)
    wkv_k = np.ascontiguousarray(wkv[:, :, :, 0:64].reshape(L, 128, 256))
    wkv_v = np.ascontiguousarray(wkv[:, :, :, 64:128].reshape(L, 128, 256))
    colv = np.zeros((L, 128, NV), f32)
    p = np.arange(128)
    p64 = (p % 64 + 32) % 64
    qn, kn = np.asarray(inp["gqa_q_norm"], f32), np.asarray(inp["gqa_k_norm"], f32)
    colv[:, :, 0] = qn[:, p % 64]
    colv[:, :, 1] = qn[:, p64]
    colv[:, :, 2] = kn[:, p % 64]
    colv[:, :, 3] = kn[:, p64]
    mq = np.asarray(inp["mla_q_norm"], f32)
    colv[:, :, 4] = mq[:, 0:128]
    colv[:, :, 5] = mq[:, 128:256]
    colv[:, :, 6] = np.asarray(inp["mla_kv_norm"], f32)
    cw = np.asarray(inp["ssd_conv_w"], f32)
    cb = np.asarray(inp["ssd_conv_b"], f32)
    for t in range(6):
        for k in range(5):
            colv[:, :, 7 + t * 5 + k] = cw[:, k, t * 128:(t + 1) * 128]
        colv[:, :, 37 + t] = cb[:, t * 128:(t + 1) * 128]
    sd = np.asarray(inp["ssd_d"], f32)
    snw = np.asarray(inp["ssd_norm_w"], f32)
    for tl in range(2):
        colv[:, :, 43 + tl] = sd[:, tl * 2 + p // 64]
        colv[:, :, 45 + tl] = snw[:, tl * 128:(tl + 1) * 128]
    cos64, sin64, cos96, sin96, consts = _tables()
    return {
        "w_mod": np.asarray(inp["w_mod"], f32), "b_mod": np.asarray(inp["b_mod"], f32),
        "norm_w": np.asarray(inp["norm_w"], f32), "w_in_p": w_in_p, "wq_up_p": wq_up_p,
        "wkv_k": wkv_k, "wkv_v": wkv_v, "colv": colv,
        "a_log": np.ascontiguousarray(np.asarray(inp["ssd_a_log"], f32).reshape(L, 8)),
        "dt_bias": np.ascontiguousarray(np.asarray(inp["ssd_dt_bias"], f32).reshape(L, 8)),
        "sink": np.asarray(inp["swa_sink"], f32), "w_out": np.asarray(inp["w_out"], f32),
        "final_norm_w": np.asarray(inp["final_norm_w"], f32), "consts": consts,
        "cos64": cos64, "sin64": sin64, "cos96": cos96, "sin96": sin96,
    }


def make_in_maps(inp, ncores=8):
    shared = _prep_shared(inp)
    x = np.asarray(inp["x"], np.float32)
    ctx = np.asarray(inp["ctx"], np.float32)
    c = np.asarray(inp["c"], np.float32)
    c_ctx = np.asarray(inp["c_ctx"], np.float32)
    maps = []
    for i in range(ncores):
        b = i % 4
        ccT = np.ascontiguousarray(np.stack([c[b], c_ctx], axis=-1).reshape(8, 128, 2).transpose(1, 0, 2))
        m = dict(shared)
        m.update({"x": np.ascontiguousarray(x[b]), "ctx": np.ascontiguousarray(ctx[b]), "ccT": ccT})
        maps.append(m)
    return maps


def kernel(**inputs):
    if "nc" not in _CACHE:
        _CACHE["nc"] = build()
    nc = _CACHE["nc"]
    maps = make_in_maps(inputs)
    res = run_bass_kernel_spmd(nc, maps, core_ids=list(range(8)))
    out = np.stack([np.asarray(res.results[b]["out"], np.float32) for b in range(4)], axis=0)
    return out
```
